# Optimizing a Trainium2 kernel written in Bass

```python
import jax, jax.numpy as jnp
from jax import lax
import numpy as np

D_MODEL = 1024
BATCH = 2
SEQ = 8192
DEPTH = 2

CTX_LEN = 256
GRID_W = 64
HEAD_DIM = 64
N_HEADS = D_MODEL // HEAD_DIM
CONV_CH = D_MODEL // 2
FOURIER_CH = D_MODEL // 2
FOURIER_GROUP = 64
CONV_WIDTH = 3
WIN_ROWS = 8
WIN_COLS = 16
D_FF = 4 * D_MODEL
N_MOD = 6
EPS = 1e-6

kernel_name = "hybrid_conv_fourier_natten_dit_trunk"


def _rms(x, g):
    xf = x.astype(jnp.float32)
    y = xf * lax.rsqrt(jnp.mean(xf * xf, axis=-1, keepdims=True) + EPS)
    return y.astype(x.dtype) * g


def _modulate(h, shift, scale):
    return h * (1 + scale) + shift


def _conv3(z, w):
    zp = jnp.pad(z, ((0, 0), (1, 1), (0, 0)))
    return zp[:, :-2] * w[0] + zp[:, 1:-1] * w[1] + zp[:, 2:] * w[2]


def _conv_fourier_mixer(h, w_in, conv_w, w_out):
    b, L, _ = h.shape
    u = h @ w_in
    a_x, a_c, a_b, f = jnp.split(u, [CONV_CH, 2 * CONV_CH, 3 * CONV_CH], axis=-1)
    a_y = a_b * _conv3(a_c * a_x, conv_w)
    fg = f.reshape(b, L, FOURIER_CH // FOURIER_GROUP, FOURIER_GROUP).astype(jnp.float32)
    f_y = jnp.fft.fftn(fg, axes=(1, 3), norm="ortho").real
    f_y = f_y.reshape(b, L, FOURIER_CH).astype(h.dtype)
    return jnp.concatenate([a_y, f_y], axis=-1) @ w_out


def _heads(h, w, g=None):
    b, n, _ = h.shape
    t = (h @ w).reshape(b, n, -1, HEAD_DIM)
    return t if g is None else _rms(t, g)


def _neighbourhood_attention(q, k, v, k_ctx, v_ctx, rpb):
    b, n, h, d = q.shape
    rows = n // GRID_W
    kr = min(WIN_ROWS, rows)
    qg = q.reshape(b, rows, GRID_W, h, d)
    kg = k.reshape(b, rows, GRID_W, h, d)
    vg = v.reshape(b, rows, GRID_W, h, d)
    cols = jnp.arange(GRID_W)
    col_idx = jnp.clip(cols - WIN_COLS // 2, 0, GRID_W - WIN_COLS)[:, None] + jnp.arange(WIN_COLS)
    col_off = col_idx - cols[:, None] + (WIN_COLS - 1)
    bias_cols = rpb[:, :, col_off]
    scale = d ** -0.5
    n_loc = kr * WIN_COLS

    def row_block(r):
        rs = jnp.clip(r - kr // 2, 0, rows - kr)
        q_r = lax.dynamic_index_in_dim(qg, r, axis=1, keepdims=False)
        k_win = lax.dynamic_slice_in_dim(kg, rs, kr, axis=1)[:, :, col_idx]
        v_win = lax.dynamic_slice_in_dim(vg, rs, kr, axis=1)[:, :, col_idx]
        row_off = rs + jnp.arange(kr) - r + (WIN_ROWS - 1)
        bias = jnp.take(bias_cols, row_off, axis=1)
        s_loc = jnp.einsum('bwhd,brwchd->bhwrc', q_r, k_win).astype(jnp.float32) * scale
        s_loc = s_loc + jnp.transpose(bias, (0, 2, 1, 3)).astype(jnp.float32)[None]
        s_ctx = jnp.einsum('bwhd,bnhd->bhwn', q_r, k_ctx).astype(jnp.float32) * scale
        s = jnp.concatenate([s_loc.reshape(b, h, GRID_W, n_loc), s_ctx], axis=-1)
        p = jax.nn.softmax(s, axis=-1).astype(v.dtype)
        p_loc = p[..., :n_loc].reshape(b, h, GRID_W, kr, WIN_COLS)
        p_ctx = p[..., n_loc:]
        return (jnp.einsum('bhwrc,brwchd->bwhd', p_loc, v_win)
                + jnp.einsum('bhwn,bnhd->bwhd', p_ctx, v_ctx))

    o = lax.map(row_block, jnp.arange(rows))
    return jnp.moveaxis(o, 0, 1).reshape(b, n, h * d)


def _context_attention(q, k, v):
    b, n, h, d = q.shape
    s = jnp.einsum('bqhd,bkhd->bhqk', q, k).astype(jnp.float32) * (d ** -0.5)
    p = jax.nn.softmax(s, axis=-1).astype(v.dtype)
    return jnp.einsum('bhqk,bkhd->bqhd', p, v).reshape(b, n, h * d)


def _mlp(h, w1, w2):
    return jnp.square(jax.nn.relu(h @ w1)) @ w2


def setup_inputs(seed: int = 0) -> dict:
    key = jax.random.key(seed)
    ks = jax.random.split(key, 20)
    n_even = (DEPTH + 1) // 2
    n_odd = DEPTH // 2
    f32 = jnp.float32

    def nrm(k, shape, s):
        return jax.random.normal(k, shape, f32) * s

    return {
        "x": nrm(ks[0], (BATCH, SEQ, D_MODEL), 1.0),
        "c": nrm(ks[1], (BATCH, D_MODEL), 1.0),
        "ctx": nrm(ks[2], (BATCH, CTX_LEN, D_MODEL), 1.0),
        "c_ctx": nrm(ks[3], (D_MODEL,), 1.0),
        "ada_w": nrm(ks[4], (DEPTH, D_MODEL, N_MOD * D_MODEL), 0.5 * D_MODEL ** -0.5),
        "ada_b": nrm(ks[5], (DEPTH, N_MOD * D_MODEL), 0.02),
        "norm_mix_g": 1.0 + nrm(ks[6], (DEPTH, D_MODEL), 0.02),
        "norm_mlp_g": 1.0 + nrm(ks[7], (DEPTH, D_MODEL), 0.02),
        "mlp_w1": nrm(ks[8], (DEPTH, D_MODEL, D_FF), D_MODEL ** -0.5),
        "mlp_w2": nrm(ks[9], (DEPTH, D_FF, D_MODEL), D_FF ** -0.5),
        "ab_w_in": nrm(ks[10], (n_even, D_MODEL, 3 * CONV_CH + FOURIER_CH), D_MODEL ** -0.5),
        "ab_conv_w": nrm(ks[11], (n_even, CONV_WIDTH, CONV_CH), CONV_WIDTH ** -0.5),
        "ab_w_out": nrm(ks[12], (n_even, CONV_CH + FOURIER_CH, D_MODEL), (CONV_CH + FOURIER_CH) ** -0.5),
        "na_w_qkv": nrm(ks[13], (n_odd, D_MODEL, 3 * N_HEADS * HEAD_DIM), D_MODEL ** -0.5),
        "na_q_g": 1.0 + nrm(ks[14], (n_odd, HEAD_DIM), 0.02),
        "na_k_g": 1.0 + nrm(ks[15], (n_odd, HEAD_DIM), 0.02),
        "na_rpb": nrm(ks[16], (n_odd, N_HEADS, 2 * WIN_ROWS - 1, 2 * WIN_COLS - 1), 0.1),
        "na_w_out": nrm(ks[17], (n_odd, N_HEADS * HEAD_DIM, D_MODEL), (N_HEADS * HEAD_DIM) ** -0.5),
    }


def reference(x, c, ctx, c_ctx, ada_w, ada_b, norm_mix_g, norm_mlp_g, mlp_w1, mlp_w2,
              ab_w_in, ab_conv_w, ab_w_out, na_w_qkv, na_q_g, na_k_g, na_rpb, na_w_out):
    hd = N_HEADS * HEAD_DIM
    cond_x = jax.nn.silu(c)
    cond_c = jax.nn.silu(c_ctx)[None]
    for i in range(DEPTH):
        last = i == DEPTH - 1
        j = i // 2
        sx1, cx1, gx1, sx2, cx2, gx2 = jnp.split((cond_x @ ada_w[i] + ada_b[i])[:, None, :], N_MOD, axis=-1)
        sc1, cc1, gc1, sc2, cc2, gc2 = jnp.split((cond_c @ ada_w[i] + ada_b[i])[:, None, :], N_MOD, axis=-1)
        hx = _modulate(_rms(x, norm_mix_g[i]), sx1, cx1)
        if i % 2 == 0:
            x = x + gx1 * _conv_fourier_mixer(hx, ab_w_in[j], ab_conv_w[j], ab_w_out[j])
            if not last:
                hc = _modulate(_rms(ctx, norm_mix_g[i]), sc1, cc1)
                ctx = ctx + gc1 * _conv_fourier_mixer(hc, ab_w_in[j], ab_conv_w[j], ab_w_out[j])
        else:
            w = na_w_qkv[j]
            hc = _modulate(_rms(ctx, norm_mix_g[i]), sc1, cc1)
            q = _heads(hx, w[:, :hd], na_q_g[j])
            k = _heads(hx, w[:, hd:2 * hd], na_k_g[j])
            v = _heads(hx, w[:, 2 * hd:])
            k_c = _heads(hc, w[:, hd:2 * hd], na_k_g[j])
            v_c = _heads(hc, w[:, 2 * hd:])
            x = x + gx1 * (_neighbourhood_attention(q, k, v, k_c, v_c, na_rpb[j]) @ na_w_out[j])
            if not last:
                q_c = _heads(hc, w[:, :hd], na_q_g[j])
                ctx = ctx + gc1 * (_context_attention(q_c, k_c, v_c) @ na_w_out[j])
        x = x + gx2 * _mlp(_modulate(_rms(x, norm_mlp_g[i]), sx2, cx2), mlp_w1[i], mlp_w2[i])
        if not last:
            ctx = ctx + gc2 * _mlp(_modulate(_rms(ctx, norm_mlp_g[i]), sc2, cc2), mlp_w1[i], mlp_w2[i])
    return x
```

```python
import numpy as np
from contextlib import ExitStack
import concourse.bass as bass
import concourse.mybir as mybir
from concourse.bass_utils import run_bass_kernel_spmd

F32 = mybir.dt.float32
BF16 = mybir.dt.bfloat16
AF = mybir.ActivationFunctionType
ALU = mybir.AluOpType
AX = mybir.AxisListType

NCORES = 8
D = 1024
NEXT = 20
NCTX = 2
NALL = 22
NTE = NEXT * 128
EPS = 1e-6


class Tok:
    __slots__ = ("key", "count", "clock")

    def __init__(self, key, count, clock):
        self.key = key
        self.count = count
        self.clock = clock


class K:
    def __init__(self, nc, es, n_dma_sems=48):
        self.nc = nc
        self.eng = {"pe": nc.tensor, "act": nc.scalar, "dve": nc.vector,
                    "pool": nc.gpsimd, "sp": nc.sync}
        self.sem, self.cnt, self.obs = {}, {}, {}
        for n in self.eng:
            self.sem[n] = es.enter_context(nc.semaphore("s_" + n))
            self.cnt[n] = 0
            self.obs[n] = {}
        self.dma_sems = {}
        self.dma_rr = {}
        self.dma_last = {}
        for q, n in (("sp", 32), ("act", 10), ("pool", 16), ("bg", 16)):
            self.dma_sems[q] = []
            self.dma_rr[q] = 0
            for i in range(n):
                kk = ("dma", q, i)
                self.sem[kk] = es.enter_context(nc.semaphore("s_dma_%s%d" % (q, i)))
                self.cnt[kk] = 0
                self.dma_sems[q].append(kk)
                self.dma_last[kk] = None
        self.res = {}
        self.pe_pending = []
        self.n_ops = 0
        self.n_waits = 0

    def _deps(self, reads, writes):
        raw, other = [], []
        for r in reads:
            e = self.res.get(r)
            if e is not None and e[0] is not None:
                raw.append(e[0])
        for w in writes:
            e = self.res.get(w)
            if e is not None:
                if e[0] is not None:
                    other.append(e[0])
                other.extend(e[1])
        return raw, other

    def _wait(self, en, tok):
        assert tok.count is not None, "dependency on unsignaled PE op"
        ob = self.obs[en]
        if ob.get(tok.key, 0) >= tok.count:
            return
        self.eng[en].wait_ge(self.sem[tok.key], tok.count)
        self.n_waits += 1
        for kk, v in tok.clock.items():
            if ob.get(kk, 0) < v:
                ob[kk] = v
        ob[tok.key] = tok.count

    def _record(self, tok, reads, writes):
        for r in reads:
            e = self.res.setdefault(r, [None, []])
            e[1].append(tok)
        for w in writes:
            self.res[w] = [tok, []]

    def op(self, en, fn, reads=(), writes=(), signal=True):
        raw, other = self._deps(reads, writes)
        for t in raw:
            if t.key == en and en == "pe":
                continue
            self._wait(en, t)
        for t in other:
            if t.key == en:
                continue
            self._wait(en, t)
        ins = fn(self.eng[en])
        self.n_ops += 1
        if signal:
            self.cnt[en] += 1
            ins.then_inc(self.sem[en], 1)
            tok = Tok(en, self.cnt[en], dict(self.obs[en]))
            if en == "pe":
                for p in self.pe_pending:
                    p.count = tok.count
                    p.clock = tok.clock
                self.pe_pending = []
        else:
            assert en == "pe"
            tok = Tok(en, None, None)
            self.pe_pending.append(tok)
        self._record(tok, reads, writes)
        return tok

    def dma(self, q, out, in_, reads=(), writes=(), bg=False, **kw):
        raw, other = self._deps(reads, writes)
        for t in raw + other:
            self._wait(q, t)
        pq = "bg" if bg else q
        kk = self.dma_sems[pq][self.dma_rr[pq]]
        self.dma_rr[pq] = (self.dma_rr[pq] + 1) % len(self.dma_sems[pq])
        prev = self.dma_last[kk]
        if prev is not None:
            self._wait(q, prev)
        self.cnt[kk] += 16
        self.eng[q].dma_start(out=out, in_=in_, **kw).then_inc(self.sem[kk], 16)
        self.n_ops += 1
        tok = Tok(kk, self.cnt[kk], dict(self.obs[q]))
        self.dma_last[kk] = tok
        self._record(tok, reads, writes)
        return tok

    def barrier(self):
        assert not self.pe_pending
        toks = []
        for en in self.eng:
            if self.cnt[en] > 0:
                toks.append(Tok(en, self.cnt[en], {}))
        for q in ("sp", "act", "pool"):
            for kk in self.dma_sems[q]:
                if self.dma_last[kk] is not None:
                    toks.append(self.dma_last[kk])
        for en in self.eng:
            for t in toks:
                if t.key != en:
                    self._wait(en, t)

    def finish(self, en="sp"):
        for q in self.dma_sems:
            for kk in self.dma_sems[q]:
                t = self.dma_last[kk]
                if t is not None:
                    self._wait(en, t)


def run_pipeline(n, stages, skews):
    st = [dict() for _ in range(n)]
    for s_ in range(n + max(skews)):
        for f, sk in zip(stages, skews):
            it = s_ - sk
            if 0 <= it < n:
                f(it, st[it])


class Ring:
    def __init__(self, name, t, n):
        self.name, self.t, self.n, self.i = name, t, n, 0

    def next(self):
        i = self.i
        self.i = (i + 1) % self.n
        return i, (self.name, i)


def build(dbg=None, stop=None):
    dbg = dbg or set()
    nc = bass.Bass("TRN2", target_bir_lowering=False)

    def din(name, shape, dt=F32):
        return nc.dram_tensor(name, list(shape), dt, kind="ExternalInput").ap()

    def dscr(name, shape, dt=BF16):
        return nc.dram_tensor(name, list(shape), dt, kind="Internal").ap()

    def dout(name, shape, dt=F32):
        return nc.dram_tensor(name, list(shape), dt, kind="ExternalOutput").ap()

    xf = din("xf", [128, 64, D])
    ctxb = din("ctxb", [256, D])
    condT = din("condT", [128, 8, 2])
    ada_w = din("ada_w", [2, D, 6 * D])
    ada_b = din("ada_b", [2, 6 * D])
    nmix = din("nmix", [2, D])
    nmlp = din("nmlp", [2, D])
    w1 = din("w1", [2, D, 4 * D])
    w2 = din("w2", [2, 4 * D, D])
    w_in = din("w_in", [D, 2 * D])
    convw = din("convw", [128, 4, 3])
    w_abo = din("w_abo", [D, D])
    w_qkv = din("w_qkv", [D, 3 * D])
    qkg = din("qkg", [128, 2])
    w_nao = din("w_nao", [D, D])
    ident_d = din("ident", [128, 128])
    sel_d = din("sel", [2, 2, 128])
    MT_d = din("MT", [128, 64, 2, 128])
    WC_d = din("WC", [128, 2, 20])
    BCS_d = din("BCS", [128, 2, 128])
    CT_d = din("CT256", [128, 2, 2, 256])
    gm_d = din("gmask", [128, 2])
    EBB_d = din("ebias", [128, 16, 12, 128])
    EBM_d = din("emask", [128, 12, 128])
    RM_d = din("rmask", [128, 4, 6, 128])
    y = dout("y", [2048, D])

    w1b = dscr("w1b", [2, D, 4 * D])
    w2b = dscr("w2b", [2, 4 * D, D])
    Gd = dscr("Gd", [128, 4, 128, 128])
    QTd = dscr("QTd", [8, 128, 2048])
    KTd = dscr("KTd", [8, 128, NALL * 128])
    Vd = dscr("Vd", [8, NALL, 128, 130])
    EBd = dscr("EBd", [8, 128, 2, 12, 128])
    xs_d = dscr("xs_d", [NALL, 128, D], F32)
    wabo_b = dscr("wabo_b", [D, D])
    wnao_b = dscr("wnao_b", [D, D])
    wqkv_b = dscr("wqkv_b", [D, 3 * D])
    ada1_b = dscr("ada1_b", [D, 6 * D])

    dbg_out = {}

    with ExitStack() as es:
        k = K(nc, es)

        sb_cnt = [0]

        def sb(name, shape, dt, stack=None):
            sb_cnt[0] += 1
            return (stack or es).enter_context(nc.sbuf_tensor("%s_%d" % (name, sb_cnt[0]), list(shape), dt))

        banks = [es.enter_context(nc.psum_tensor("ps%d" % i, [128, 512], F32)) for i in range(8)]
        bank_rr = [0]

        def nbank():
            i = bank_rr[0]
            bank_rr[0] = (i + 1) % 8
            return i

        def bkey(i):
            return ("ps", i)

        identb = sb("identb", [128, 128], BF16)
        identf = sb("identf", [128, 128], F32)
        sel2f = sb("sel2f", [2, 2, 128], F32)
        sel2b = sb("sel2b", [2, 2, 128], BF16)
        epsc = sb("epsc", [128, 1], F32)
        junk = sb("junk", [128, D], BF16)
        stat = sb("stat", [128, 8, 2], F32)
        stat_ring = Ring("stat", stat, 8)
        xs_t = sb("xs_t", [128, 5, D], BF16)
        xs_ring = Ring("xs", xs_t, 5)
        condf = sb("condf", [128, 8, 2], F32)
        condb = sb("condb", [128, 8, 2], BF16)
        shT_t = sb("shT_t", [128, 2, 2, 8, 2], BF16)
        rrow_t = sb("rrow", [2, 4, 512], F32)
        bcol_tmp = sb("bcol_tmp", [2, 2, 512], F32)

        k.dma("sp", identf[:], ident_d[:, :], writes=["identf"])
        k.dma("pool", identb[:], ident_d[:, :], writes=["identb"])
        k.dma("sp", sel2f[:], sel_d[:, :, :], writes=["sel2f"])
        k.dma("pool", sel2b[:], sel_d[:, :, :], writes=["sel2b"])
        k.dma("sp", condf[:], condT[:, :, :], writes=["condf"])
        k.op("dve", lambda e: e.memset(epsc[:], EPS), writes=["epsc"])
        k.op("act", lambda e: e.activation(out=condb[:], in_=condf[:], func=AF.Silu),
             reads=["condf"], writes=["condb"])

        def convert_mlp(l):
            for r0 in range(0, D, 256):
                k.dma("pool", w1b[l, r0:r0 + 256, :], w1[l, r0:r0 + 256, :], writes=[("w1b", l, r0 // 256)], bg=True)
            for r0 in range(0, 4 * D, 512):
                k.dma("pool", w2b[l, r0:r0 + 512, :], w2[l, r0:r0 + 512, :], writes=[("w2b", l, r0 // 512)], bg=True)

        def convert_bg(dst, src, key, rows_per):
            R_ = src.shape[0]
            for r0 in range(0, R_, rows_per):
                k.dma("pool", dst[r0:r0 + rows_per, :], src[r0:r0 + rows_per, :], writes=[(key, r0 // rows_per)], bg=True)
            return [(key, i) for i in range(R_ // rows_per)]

        def load_w_bf16(dst, src2d, key, ckeys, rows_per=256):
            R_ = src2d.shape[0]
            for r0 in range(0, R_, rows_per):
                c0, c1 = r0 // 128, (r0 + rows_per) // 128
                k.dma("sp", dst[:, c0:c1, :], src2d[r0:r0 + rows_per, :].rearrange("(c p) n -> p c n", p=128),
                      reads=ckeys, writes=[(key, c) for c in range(c0, c1)])

        def load_w_cast(dst, src2d, key, rows_per=256):
            R = src2d.shape[0]
            for r0 in range(0, R, rows_per):
                c0, c1 = r0 // 128, (r0 + rows_per) // 128
                k.dma("pool", dst[:, c0:c1, :],
                      src2d[r0:r0 + rows_per, :].rearrange("(c p) n -> p c n", p=128),
                      writes=[(key, c) for c in range(c0, c1)])

        evac_rr = [0]

        def evac_eng():
            evac_rr[0] ^= 1
            return "act" if evac_rr[0] else "dve"

        def copy_op(en, out, in_, reads, writes):
            if en == "act":
                return k.op("act", lambda e: e.copy(out=out, in_=in_), reads=reads, writes=writes)
            return k.op(en, lambda e: e.tensor_copy(out=out, in_=in_), reads=reads, writes=writes)

        def norm_tile(x_ap, xkey, S_ap, skey, dst_fn):
            si, skey_stat = stat_ring.next()
            ssq = stat[:, si, 0:1]
            rstd = stat[:, si, 1:2]
            k.op("act", lambda e: e.activation(out=junk[:], in_=x_ap, func=AF.Square, accum_out=ssq),
                 reads=[xkey], writes=["junk", skey_stat])
            k.op("act", lambda e: e.activation(out=rstd, in_=ssq, func=AF.Ln, scale=1.0 / D, bias=epsc[:, 0:1]),
                 reads=[skey_stat, "epsc"], writes=[skey_stat])
            k.op("act", lambda e: e.activation(out=rstd, in_=rstd, func=AF.Exp, scale=-0.5),
                 reads=[skey_stat], writes=[skey_stat])
            xi, xskey = xs_ring.next()
            xs = xs_t[:, xi, :]
            k.op("dve", lambda e: e.scalar_tensor_tensor(out=xs, in0=x_ap, scalar=rstd, in1=S_ap,
                                                         op0=ALU.mult, op1=ALU.mult),
                 reads=[xkey, skey_stat, skey], writes=[xskey])
            b = nbank()
            bv = banks[b][:].bitcast(BF16).rearrange("p (c t) -> p c t", c=8)
            for c in range(8):
                k.op("pe", lambda e, c=c: e.transpose(bv[:, c, :], xs[:, c * 128:(c + 1) * 128], identb[:]),
                     reads=[xskey, "identb"], writes=[bkey(b)], signal=(c == 7))
            dst_fn(bv, bkey(b))

        shT = {}
        BC = {}
        SEG = ["s1", "c1", "g1", "s2", "c2", "g2"]
        for l in range(2):
            for si_, n in enumerate(("s1", "s2")):
                shT[(l, n)] = shT_t[:, l, si_, :, :]

        def modulation(l, segs, variants, bc_alloc, src_bf16=None, src_keys=()):
            with ExitStack() as ms:
                A_t = sb("A_t", [128, 2, 8, 512], BF16, ms)
                A_ring = Ring(("A", l, segs[0]), A_t, 2)
                rr = [0]
                for seg in segs:
                    n = SEG[seg]
                    for half in range(2):
                        ng = seg * 2 + half
                        ai, akey = A_ring.next()
                        if src_bf16 is None:
                            k.dma("pool", A_t[:, ai, :, :],
                                  ada_w[l, :, ng * 512:(ng + 1) * 512].rearrange("(c p) n -> p c n", p=128),
                                  writes=[akey])
                        else:
                            k.dma("sp", A_t[:, ai, :, :],
                                  src_bf16[:, ng * 512:(ng + 1) * 512].rearrange("(c p) n -> p c n", p=128),
                                  reads=list(src_keys), writes=[akey])
                        k.dma("sp", rrow_t[:, 2, :], ada_b[l:l + 1, ng * 512:(ng + 1) * 512].partition_broadcast(2),
                              writes=["adab"])
                        b = nbank()
                        for c in range(8):
                            k.op("pe", lambda e, c=c: e.matmul(banks[b][0:2, :], condb[:, c, :], A_t[:, ai, c, :],
                                                              start=(c == 0), stop=(c == 7)),
                                 reads=["condb", akey], writes=[bkey(b)], signal=(c == 7))
                        ri = rr[0]
                        rr[0] ^= 1
                        rkey = ("rrow", ri)
                        r = rrow_t[:, ri, :]
                        k.op("dve", lambda e: e.tensor_tensor(out=r, in0=banks[b][0:2, :], in1=rrow_t[:, 2, :], op=ALU.add),
                             reads=[bkey(b), "adab"], writes=[rkey])
                        if n in ("c1", "c2"):
                            gsrc = nmix if n == "c1" else nmlp
                            k.dma("sp", rrow_t[:, 3, :],
                                  gsrc[l:l + 1, half * 512:(half + 1) * 512].partition_broadcast(2), writes=["gainp"])
                            k.op("dve", lambda e: e.scalar_tensor_tensor(out=r, in0=r, scalar=1.0, in1=rrow_t[:, 3, :],
                                                                         op0=ALU.add, op1=ALU.mult),
                                 reads=[rkey, "gainp"], writes=[rkey])
                        if n in ("s1", "s2"):
                            b2 = nbank()
                            for j in range(4):
                                k.op("pe", lambda e, j=j: e.transpose(banks[b2][:, j * 2:(j + 1) * 2],
                                                                      r[0:2, j * 128:(j + 1) * 128], identf[0:2, 0:2]),
                                     reads=[rkey, "identf"], writes=[bkey(b2)], signal=(j == 3))
                            dst = shT[(l, n)]
                            k.op("act", lambda e: e.copy(out=dst[:, half * 4:(half + 1) * 4, :],
                                                         in_=banks[b2][:, 0:8].rearrange("p (j v) -> p j v", v=2)),
                                 reads=[bkey(b2)], writes=[("shT", l, n)])
                        else:
                            for v in variants:
                                ap_, key_ = bc_alloc(n, v)
                                b2 = nbank()
                                k.op("pe", lambda e, v=v: e.matmul(banks[b2][:, :], sel2f[:, v, :], r, start=True, stop=True),
                                     reads=["sel2f", rkey], writes=[bkey(b2)])
                                copy_op(evac_eng(), ap_[:, half * 512:(half + 1) * 512], banks[b2][:, :],
                                        [bkey(b2)], [key_])
                k.barrier()

        def bias_rows(shcol, shkey, w_ap_fn, wkeys, ncols, out_fn):
            for g in range((ncols + 511) // 512):
                wdt = min(512, ncols - g * 512)
                b = nbank()
                for c in range(8):
                    k.op("pe", lambda e, c=c: e.matmul(banks[b][0:2, 0:wdt], shcol[:, c, :], w_ap_fn(c, g),
                                                      start=(c == 0), stop=(c == 7)),
                         reads=[shkey] + list(wkeys(g)), writes=[bkey(b)], signal=(c == 7))
                out_fn(g, banks[b][0:2, 0:wdt], bkey(b))

        bcol_rr = [0]

        def rows_to_cols(row_ap, row_key, width, dst_cols, dkey, c0):
            ti = bcol_rr[0]
            bcol_rr[0] ^= 1
            tkey = ("bcol_tmp", ti)
            k.op("dve", lambda e: e.tensor_copy(out=bcol_tmp[:, ti, 0:width], in_=row_ap), reads=[row_key], writes=[tkey])
            b2 = nbank()
            nj = width // 128
            for j in range(nj):
                k.op("pe", lambda e, j=j: e.transpose(banks[b2][:, j * 2:(j + 1) * 2],
                                                      bcol_tmp[0:2, ti, j * 128:(j + 1) * 128], identf[0:2, 0:2]),
                     reads=[tkey, "identf"], writes=[bkey(b2)], signal=(j == nj - 1))
            k.op("act", lambda e: e.copy(out=dst_cols[:, c0:c0 + nj, :],
                                         in_=banks[b2][:, 0:2 * nj].rearrange("p (j v) -> p j v", v=2)),
                 reads=[bkey(b2)], writes=[dkey])

        def dbg_dump(name, src_ap, shape, dt, reads):
            if name in dbg:
                o = dout("dbg_" + name, shape, dt)
                k.dma("sp", o, src_ap, reads=reads, writes=["dbg_" + name])

        L0 = es.enter_context(ExitStack())
        MIX = es.enter_context(ExitStack())
        bc0_t = sb("bc0", [128, 4, D], F32, MIX)
        bc0_names = {}

        def bc0_alloc(n, v):
            kk_ = (n, v)
            if kk_ not in bc0_names:
                bc0_names[kk_] = len(bc0_names)
            i = bc0_names[kk_]
            BC[(0, n, v)] = (bc0_t[:, i, :], ("bc0", i))
            return BC[(0, n, v)]

        modulation(0, [0, 1], [0, 1], bc0_alloc)

        win_t = sb("win_t", [128, 8, 2 * D], BF16, MIX)
        load_w_cast(win_t, w_in, "win")
        WIN_KEYS = [("win", c) for c in range(8)]
        modulation(0, [2], [0, 1], bc0_alloc)
        WABO_KEYS = convert_bg(wabo_b, w_abo, "wabo_b", 512)
        convert_mlp(0)

        bin_cols = sb("bin_cols", [128, 12, 2], F32, MIX)
        bf_row = sb("bf_row", [2, 512], BF16, MIX)
        AY = sb("AY", [128, 8, NTE], BF16, MIX)
        AYc = sb("AYc", [128, 8, 256], BF16, MIX)

        def _bin_out(g, row_ap, row_key):
            if g < 3:
                rows_to_cols(row_ap, row_key, 512, bin_cols, "bin_cols", g * 4)
            else:
                k.op("act", lambda e: e.copy(out=bf_row[:, :], in_=row_ap), reads=[row_key], writes=["bf_row"])

        bias_rows(shT[(0, "s1")], ("shT", 0, "s1"), lambda c, g: win_t[:, c, g * 512:(g + 1) * 512],
                  lambda g: WIN_KEYS, 2 * D, _bin_out)

        WC_t = sb("WC_t", [128, 2, 20], BF16, MIX)
        BCS_t = sb("BCS_t", [128, 2, 128], BF16, MIX)
        CT_t = sb("CT_t", [128, 2, 2, 256], BF16, MIX)
        cw_t = sb("cw_t", [128, 4, 3], F32, MIX)
        gm_t = sb("gm_t", [128, 2], F32, MIX)
        k.dma("pool", WC_t[:], WC_d[:, :, :], writes=["WC"])
        k.dma("pool", BCS_t[:], BCS_d[:, :, :], writes=["BCS"])
        k.dma("pool", CT_t[:], CT_d[:, :, :, :], writes=["CT"])
        k.dma("sp", cw_t[:], convw[:, :, :], writes=["cw"])
        k.dma("sp", gm_t[:], gm_d[:, :], writes=["gm"])

        CD = es.enter_context(ExitStack())
        hTe = sb("hTe", [128, 8, NTE + 2], BF16, CD)
        PC = es.enter_context(ExitStack())
        MT_t = sb("MT_t", [128, 6, 2, 128], BF16, PC)
        MT_ring = Ring("MT", MT_t, 6)
        xc_t = sb("xc_t", [128, 3, D], F32, PC)
        xc_ring = Ring("xc", xc_t, 3)
        hTt = sb("hTt", [128, 3, 8, 128], BF16, PC)
        hTt_ring = Ring("hTt", hTt, 3)
        f_t = sb("f_t", [128, 3, 512], BF16, PC)
        f_ring = Ring("f", f_t, 3)
        G_t = sb("G_t", [128, 3, 2, 512], BF16, PC)
        G_ring = Ring("G", G_t, 3)
        S1x, S1xk = BC[(0, "c1", 0)]
        k.op("pool", lambda e: e.memset(hTe[:, :, NTE + 1:NTE + 2], 0.0), writes=["hTe_tail"])

        def c_s0(t2, st):
            xi, xkey = xc_ring.next()
            k.dma("sp", xc_t[:, xi, :], xf[:, t2, :], writes=[xkey])
            mi, mkey = MT_ring.next()
            k.dma("pool", MT_t[:, mi, :, :], MT_d[:, t2, :, :], writes=[mkey])
            st["mi"], st["mkey"] = mi, mkey
            x_ap = xc_t[:, xi, :]
            si, skey_stat = stat_ring.next()
            ssq = stat[:, si, 0:1]
            rstd = stat[:, si, 1:2]
            k.op("act", lambda e: e.activation(out=junk[:], in_=x_ap, func=AF.Square, accum_out=ssq),
                 reads=[xkey], writes=["junk", skey_stat])
            k.op("act", lambda e: e.activation(out=rstd, in_=ssq, func=AF.Ln, scale=1.0 / D, bias=epsc[:, 0:1]),
                 reads=[skey_stat, "epsc"], writes=[skey_stat])
            k.op("act", lambda e: e.activation(out=rstd, in_=rstd, func=AF.Exp, scale=-0.5),
                 reads=[skey_stat], writes=[skey_stat])
            qi, qkey = xs_ring.next()
            k.op("dve", lambda e: e.scalar_tensor_tensor(out=xs_t[:, qi, :], in0=x_ap, scalar=rstd, in1=S1x,
                                                         op0=ALU.mult, op1=ALU.mult),
                 reads=[xkey, skey_stat, S1xk], writes=[qkey])
            st["qi"], st["qkey"] = qi, qkey

        def c_s1(t2, st):
            qi, qkey = st["qi"], st["qkey"]
            b = nbank()
            bv = banks[b][:].bitcast(BF16).rearrange("p (c t) -> p c t", c=8)
            for c in range(8):
                k.op("pe", lambda e, c=c: e.transpose(bv[:, c, :], xs_t[:, qi, c * 128:(c + 1) * 128], identb[:]),
                     reads=[qkey, "identb"], writes=[bkey(b)], signal=(c == 7))
            hi, hkey = hTt_ring.next()
            st["hi"], st["hkey"] = hi, hkey
            copy_op("act", hTt[:, hi, :, :], bv, [bkey(b)], [hkey])
            k.op("pool", lambda e: e.tensor_copy(out=hTe[:, :, 1 + t2:1 + t2 + 64 * 39 + 1:64],
                                                 in_=hTt[:, hi, :, 0:40]), reads=[hkey], writes=[("hTe", t2)])
            if t2 == 63:
                k.op("pool", lambda e: e.tensor_copy(out=hTe[:, :, 0:1], in_=hTt[:, hi, :, 127:128]),
                     reads=[hkey], writes=["hTe_halo"])

        def c_s2(t2, st):
            hi, hkey = st["hi"], st["hkey"]
            b = nbank()
            for c in range(8):
                k.op("pe", lambda e, c=c: e.matmul(banks[b][:, :], hTt[:, hi, c, :], win_t[:, c, 1536:2048],
                                                  start=(c == 0), stop=False),
                     reads=[hkey, ("win", c)], writes=[bkey(b)], signal=False)
            k.op("pe", lambda e: e.matmul(banks[b][:, :], sel2b[:, 0, :], bf_row[:, :], start=False, stop=True),
                 reads=["sel2b", "bf_row"], writes=[bkey(b)])
            fi, fkey = f_ring.next()
            st["fi"], st["fkey"] = fi, fkey
            copy_op("dve", f_t[:, fi, :], banks[b][:, :], [bkey(b)], [fkey])

        def c_s3(t2, st):
            fi, fkey, mi, mkey = st["fi"], st["fkey"], st["mi"], st["mkey"]
            gi, gkey = G_ring.next()
            for ri in range(2):
                b2 = nbank()
                k.op("pe", lambda e, ri=ri, b2=b2: e.matmul(banks[b2][:, :], MT_t[:, mi, ri, :], f_t[:, fi, :],
                                                            start=True, stop=True),
                     reads=[fkey, mkey], writes=[bkey(b2)])
                copy_op("act" if ri == 0 else "dve", G_t[:, gi, ri, :], banks[b2][:, :], [bkey(b2)], [(gkey, ri)])
            for ri in range(2):
                k.dma("act", Gd[ri * 64 + t2, :, :, :].rearrange("c k h -> k c h"),
                      G_t[:, gi, ri, :].rearrange("p (c h) -> p c h", c=4),
                      reads=[(gkey, ri)], writes=[("Gd", ri, t2)])

        run_pipeline(64, [c_s0, c_s1, c_s2, c_s3], [0, 1, 2, 3])
        k.barrier()
        PC.close()
        HTE_KEYS = [("hTe", t) for t in range(64)] + ["hTe_halo", "hTe_tail"]
        GD_KEYS = [("Gd", ri, t) for ri in range(2) for t in range(64)]
        dbg_dump("hTe", hTe[:, :, :], [128, 8, NTE + 2], BF16, HTE_KEYS)
        dbg_dump("Gd", Gd[:, :, :, :], [128, 4, 128, 128], BF16, GD_KEYS)
        dbg_out["bc0_names"] = dict(bc0_names)
        if stop == "C":
            k.finish()
            print("ops", k.n_ops, "waits", k.n_waits)
            return nc, dbg_out
        def conv_mixer_a(hT_ap, hkeys, ntok, nz, z_bufs, ax_ring, ax_t, tmp_t, tmp_ring, v, dst, dkey_fn, ghost):
            ngrp = (ntok + 511) // 512
            nzg = (nz + 511) // 512
            def zloop(cc):
                z_t = z_bufs[cc % 2]
                zkey = ("zb", cc % 2)
                k.op("dve", lambda e: e.memset(z_t[:, nz:ntok + 2], 0.0), writes=[("ztail", cc % 2)])
                for g in range(nzg):
                    c0 = g * 512
                    wd = min(512, nz - c0)
                    bx = nbank()
                    for c in range(8):
                        k.op("pe", lambda e, c=c: e.matmul(banks[bx][:, 0:wd], win_t[:, c, cc * 128:(cc + 1) * 128],
                                                          hT_ap[:, c, c0:c0 + wd], start=(c == 0), stop=(c == 7)),
                             reads=[("win", c)] + hkeys, writes=[bkey(bx)], signal=(c == 7))
                    ai, akey = ax_ring.next()
                    k.op("act", lambda e: e.activation(out=ax_t[:, ai, 0:wd], in_=banks[bx][:, 0:wd], func=AF.Identity,
                                                       bias=bin_cols[:, cc, v:v + 1]),
                         reads=[bkey(bx), "bin_cols"], writes=[akey])
                    bc_ = nbank()
                    for c in range(8):
                        k.op("pe", lambda e, c=c: e.matmul(banks[bc_][:, 0:wd],
                                                          win_t[:, c, 512 + cc * 128:512 + (cc + 1) * 128],
                                                          hT_ap[:, c, c0:c0 + wd], start=(c == 0), stop=(c == 7)),
                             reads=[("win", c)] + hkeys, writes=[bkey(bc_)], signal=(c == 7))
                    k.op("dve", lambda e: e.scalar_tensor_tensor(out=z_t[:, c0:c0 + wd], in0=banks[bc_][:, 0:wd],
                                                                 scalar=bin_cols[:, 4 + cc, v:v + 1], in1=ax_t[:, ai, 0:wd],
                                                                 op0=ALU.add, op1=ALU.mult),
                         reads=[bkey(bc_), akey, "bin_cols"], writes=[(zkey, g)])
                zk_all = [(zkey, g) for g in range(nzg)] + [("ztail", cc % 2)]
                if ghost == "ext":
                    for gi_, col in ((0, 256), (1, 2305)):
                        k.op("dve", lambda e, gi_=gi_, col=col: e.tensor_scalar(
                            z_t[:, col:col + 1], z_t[:, col:col + 1], gm_t[:, gi_:gi_ + 1], None, op0=ALU.mult),
                            reads=zk_all + ["gm"], writes=[(zkey, col // 512)])
                else:
                    k.op("dve", lambda e: e.memset(z_t[:, 0:1], 0.0), reads=zk_all, writes=[(zkey, 0)])

            def convloop(cc):
                z_t = z_bufs[cc % 2]
                zkey = ("zb", cc % 2)
                zk_all = [(zkey, g) for g in range(nzg)] + [("ztail", cc % 2)]
                for g in range(ngrp):
                    e0 = g * 512
                    wd = min(512, ntok - e0)
                    ti, tkey = tmp_ring.next()
                    tt_ = tmp_t[:, ti, 0:wd]
                    k.op("dve", lambda e: e.tensor_scalar(tt_, z_t[:, 1 + e0:1 + e0 + wd], cw_t[:, cc, 1:2], None,
                                                           op0=ALU.mult), reads=zk_all + ["cw"], writes=[tkey])
                    k.op("dve", lambda e: e.scalar_tensor_tensor(out=tt_, in0=z_t[:, e0:e0 + wd], scalar=cw_t[:, cc, 0:1],
                                                                  in1=tt_, op0=ALU.mult, op1=ALU.add),
                         reads=zk_all + [tkey], writes=[tkey])
                    k.op("dve", lambda e: e.scalar_tensor_tensor(out=tt_, in0=z_t[:, 2 + e0:2 + e0 + wd],
                                                                  scalar=cw_t[:, cc, 2:3], in1=tt_, op0=ALU.mult, op1=ALU.add),
                         reads=zk_all + [tkey], writes=[tkey])
                    bb = nbank()
                    for c in range(8):
                        k.op("pe", lambda e, c=c: e.matmul(banks[bb][:, 0:wd],
                                                          win_t[:, c, 1024 + cc * 128:1024 + (cc + 1) * 128],
                                                          hT_ap[:, c, 1 + e0:1 + e0 + wd], start=(c == 0), stop=(c == 7)),
                             reads=[("win", c)] + hkeys, writes=[bkey(bb)], signal=(c == 7))
                    k.op("dve", lambda e: e.scalar_tensor_tensor(out=dst[:, cc, e0:e0 + wd], in0=banks[bb][:, 0:wd],
                                                                 scalar=bin_cols[:, 8 + cc, v:v + 1], in1=tt_,
                                                                 op0=ALU.add, op1=ALU.mult),
                         reads=[bkey(bb), tkey, "bin_cols"], writes=[dkey_fn(cc, g)])


            zloop(0)
            for cc in range(4):
                if cc + 1 < 4:
                    zloop(cc + 1)
                convloop(cc)

        PD = es.enter_context(ExitStack())
        z_t = sb("z_t", [128, 2, NTE + 2], BF16, PD)
        ax_t = sb("ax_t", [128, 3, 512], BF16, PD)
        ax_ring = Ring("ax", ax_t, 3)
        ctmp_t = sb("ctmp_t", [128, 3, 512], F32, PD)
        ctmp_ring = Ring("ctmp", ctmp_t, 3)
        conv_mixer_a(hTe, HTE_KEYS, NTE, NTE, [z_t[:, 0, :], z_t[:, 1, :]], ax_ring, ax_t, ctmp_t, ctmp_ring, 0, AY, lambda cc, g: ("AY", cc, g), "ext")
        k.barrier()
        PD.close()
        CD.close()
        dbg_dump("bin_cols", bin_cols[:, :, :], [128, 12, 2], F32, ["bin_cols"])
        AY_KEYS_A = [("AY", cc, g) for cc in range(4) for g in range(5)]
        dbg_dump("AYa", AY[:, 0:4, :], [128, 4, NTE], BF16, AY_KEYS_A)
        if stop == "D":
            k.finish()
            return nc, dbg_out

        PE_ = es.enter_context(ExitStack())
        R_t = sb("R_t", [128, 2, 64, 128], BF16, PE_)
        R_ring = Ring("R", R_t, 2)
        Z_t = sb("Z_t", [128, 2, 2, NTE], BF16, PE_)
        Z_ring = Ring("Z", Z_t, 2)
        for cc in range(4):
            zi, zkey = Z_ring.next()
            Zv = Z_t[:, zi, :, :].rearrange("p r (i q) -> p q r i", q=128)
            for half in range(2):
                ri_, rkey = R_ring.next()
                k.dma("sp", R_t[:, ri_, :, :], Gd[:, cc, half * 64:(half + 1) * 64, :], reads=GD_KEYS, writes=[rkey])
                for g0 in range(0, 64, 12):
                    nj = min(12, 64 - g0)
                    b = nbank()
                    for j in range(nj):
                        k.op("pe", lambda e, j=j: e.matmul(banks[b][:, j * 40:(j + 1) * 40], R_t[:, ri_, g0 + j, :],
                                                          WC_t[:].rearrange("p r i -> p (r i)"), start=True, stop=True),
                             reads=[rkey, "WC"], writes=[bkey(b)], signal=(j == nj - 1))
                    k1_0 = half * 64 + g0
                    copy_op(evac_eng(), Zv[:, k1_0:k1_0 + nj, :, :],
                            banks[b][:, 0:nj * 40].rearrange("p (j r i) -> p j r i", r=2, i=20),
                            [bkey(b)], [(zkey, half, g0)])
            zk_all = [(zkey, h_, g_) for h_ in range(2) for g_ in range(0, 64, 12)]
            for g in range(5):
                b = nbank()
                for cs in range(2):
                    k.op("pe", lambda e, cs=cs: e.matmul(banks[b][:, :], BCS_t[:, cs, :], Z_t[:, zi, cs, g * 512:(g + 1) * 512],
                                                        start=(cs == 0), stop=(cs == 1)),
                         reads=zk_all + ["BCS"], writes=[bkey(b)], signal=(cs == 1))
                copy_op(evac_eng(), AY[:, 4 + cc, g * 512:(g + 1) * 512], banks[b][:, :], [bkey(b)], [("AY", 4 + cc, g)])
        k.barrier()
        PE_.close()
        AY_KEYS = [("AY", cc, g) for cc in range(8) for g in range(5)]
        dbg_dump("AY", AY[:, :, :], [128, 8, NTE], BF16, AY_KEYS)
        if stop == "E":
            k.finish()
            return nc, dbg_out

        PG = es.enter_context(ExitStack())
        hTc = sb("hTc", [128, 8, 258], BF16, PG)
        zc_t = sb("zc_t", [128, 2, 258], BF16, PG)
        axc_t = sb("axc_t", [128, 2, 512], BF16, PG)
        axc_ring = Ring("axc", axc_t, 2)
        ctmpc_t = sb("ctmpc_t", [128, 2, 512], F32, PG)
        ctmpc_ring = Ring("ctmpc", ctmpc_t, 2)
        xg_t = sb("xg0_t", [128, 4, D], F32, PG)
        xg_ring = Ring("xg0", xg_t, 4)
        fc_t = sb("fc_t", [128, 2, 512], BF16, PG)
        Zc_t = sb("Zc_t", [128, 2, 2, 256], BF16, PG)
        Zc_ring = Ring("Zc", Zc_t, 2)
        Wo_t = sb("Wo_t", [128, 8, D], BF16, PG)
        load_w_bf16(Wo_t, wabo_b, "Wo", WABO_KEYS)
        WO_KEYS = [("Wo", c) for c in range(8)]
        S1c, S1ck = BC[(0, "c1", 1)]
        k.op("dve", lambda e: e.memset(hTc[:, :, 0:1], 0.0), writes=["hTc_pad0"])
        k.op("dve", lambda e: e.memset(hTc[:, :, 257:258], 0.0), writes=["hTc_pad1"])
        for j in range(NCTX):
            xi, xkey = xg_ring.next()
            k.dma("sp", xg_t[:, xi, :], ctxb[j * 128:(j + 1) * 128, :], writes=[xkey])

            def _evc(bv, bk, j=j):
                copy_op("act", hTc[:, :, 1 + j * 128:1 + (j + 1) * 128], bv, [bk], [("hTc", j)])
            norm_tile(xg_t[:, xi, :], xkey, S1c, S1ck, _evc)
        HTC_KEYS = [("hTc", 0), ("hTc", 1), "hTc_pad0", "hTc_pad1"]
        conv_mixer_a(hTc, HTC_KEYS, 256, 257, [zc_t[:, 0, :], zc_t[:, 1, :]], axc_ring, axc_t, ctmpc_t, ctmpc_ring, 1, AYc,
                     lambda cc, g: ("AYc", cc), "ctx")
        for j in range(NCTX):
            b = nbank()
            for c in range(8):
                k.op("pe", lambda e, c=c: e.matmul(banks[b][:, :], hTc[:, c, 1 + j * 128:1 + (j + 1) * 128],
                                                  win_t[:, c, 1536:2048], start=(c == 0), stop=False),
                     reads=[("hTc", j), ("win", c)], writes=[bkey(b)], signal=False)
            k.op("pe", lambda e: e.matmul(banks[b][:, :], sel2b[:, 1, :], bf_row[:, :], start=False, stop=True),
                 reads=["sel2b", "bf_row"], writes=[bkey(b)])
            copy_op(evac_eng(), fc_t[:, j, :], banks[b][:, :], [bkey(b)], [("fc", j)])
        for cc in range(4):
            zi, zkey = Zc_ring.next()
            b = nbank()
            for cs in range(2):
                for tt in range(2):
                    k.op("pe", lambda e, cs=cs, tt=tt: e.matmul(banks[b][:, cs * 256:(cs + 1) * 256],
                                                               fc_t[:, tt, cc * 128:(cc + 1) * 128], CT_t[:, tt, cs, :],
                                                               start=(tt == 0), stop=(tt == 1)),
                         reads=[("fc", tt), "CT"], writes=[bkey(b)], signal=(cs == 1 and tt == 1))
            copy_op(evac_eng(), Zc_t[:, zi, :, :], banks[b][:, :].rearrange("p (r q) -> p r q", r=2), [bkey(b)], [zkey])
            b2 = nbank()
            for cs in range(2):
                k.op("pe", lambda e, cs=cs: e.matmul(banks[b2][:, 0:256], BCS_t[:, cs, :], Zc_t[:, zi, cs, :],
                                                    start=(cs == 0), stop=(cs == 1)),
                     reads=[zkey, "BCS"], writes=[bkey(b2)], signal=(cs == 1))
            copy_op(evac_eng(), AYc[:, 4 + cc, :], banks[b2][:, 0:256], [bkey(b2)], [("AYc", 4 + cc)])
        AYC_KEYS = [("AYc", c) for c in range(8)]
        dbg_dump("AYc", AYc[:, :, :], [128, 8, 256], BF16, AYC_KEYS)

        gtmp_t = sb("gtmp_t", [128, 2, 512], F32, PG)
        gtmp_ring = Ring("gtmp", gtmp_t, 2)

        def out_proj_residual(src, skeys_fn, tok0, xslot, G_ap, gkey, dst_dram, dkey):
            xi, xkey = xslot
            for nh in range(2):
                b = nbank()
                for kc in range(8):
                    k.op("pe", lambda e, kc=kc: e.matmul(banks[b][:, :], src[:, kc, tok0:tok0 + 128],
                                                        Wo_t[:, kc, nh * 512:(nh + 1) * 512], start=(kc == 0), stop=(kc == 7)),
                         reads=skeys_fn(kc) + [("Wo", kc)], writes=[bkey(b)], signal=(kc == 7))
                ti, tkey = gtmp_ring.next()
                k.op("dve", lambda e: e.tensor_tensor(out=gtmp_t[:, ti, :], in0=banks[b][:, :],
                                                      in1=G_ap[:, nh * 512:(nh + 1) * 512], op=ALU.mult),
                     reads=[bkey(b), gkey], writes=[tkey])
                k.op("dve", lambda e: e.tensor_tensor(out=xg_t[:, xi, nh * 512:(nh + 1) * 512],
                                                      in0=xg_t[:, xi, nh * 512:(nh + 1) * 512], in1=gtmp_t[:, ti, :], op=ALU.add),
                     reads=[tkey, xkey], writes=[xkey])
            k.dma("act", dst_dram, xg_t[:, xi, :], reads=[xkey], writes=[dkey])

        G1x, G1xk = BC[(0, "g1", 0)]
        G1c, G1ck = BC[(0, "g1", 1)]

        def f_src(t):
            if t < NEXT:
                return xf[2 * t:2 * t + 2, :, :].rearrange("p t d -> (p t) d")
            return ctxb[(t - NEXT) * 128:(t - NEXT + 1) * 128, :]

        fslots = {}

        def f_load(t):
            xi, xkey = xg_ring.next()
            k.dma("sp", xg_t[:, xi, :], f_src(t), writes=[xkey])
            fslots[t] = (xi, xkey)

        f_load(0)
        f_load(1)
        for t in range(NALL):
            if t + 2 < NALL:
                f_load(t + 2)
            if t < NEXT:
                out_proj_residual(AY, lambda kc, t=t: [("AY", kc, t // 4)], t * 128, fslots[t], G1x, G1xk,
                                  xs_d[t, :, :], ("xs", t))
            else:
                out_proj_residual(AYc, lambda kc: [("AYc", kc)], (t - NEXT) * 128, fslots[t], G1c, G1ck,
                                  xs_d[t, :, :], ("xs", t))
        k.barrier()
        PG.close()
        MIX.close()
        XS_KEYS = [("xs", i) for i in range(NALL)]
        dbg_dump("x1m", xs_d[:, :, :], [NALL, 128, D], F32, XS_KEYS)
        if stop == "F":
            k.finish()
            return nc, dbg_out
        def mlp_layer(l, groups, S_of, G_of, dst_fn, dkey_fn, lstack):
            xm_t = sb("xm_t", [128, 8, D], F32, lstack)
            xm_ring = Ring(("xm", l), xm_t, 8)
            xsm_t = sb("xsm_t", [128, 5, D], BF16, lstack)
            xsm_ring = Ring(("xsm", l), xsm_t, 5)
            hTg = sb("hTg", [128, 2, 8, 512], BF16, lstack)
            h1T = sb("h1T", [128, 32, 512], BF16, lstack)
            W1p = sb("W1p", [128, 3, 8, 512], BF16, lstack)
            W1_ring = Ring(("W1p", l), W1p, 3)
            W2p = sb("W2p", [128, 3, 4, D], BF16, lstack)
            W2_ring = Ring(("W2p", l), W2p, 3)
            rt_t = sb("rt_t", [128, 3, 512], BF16, lstack)
            rt_ring = Ring(("rt", l), rt_t, 3)
            mt_t = sb("mt_t", [128, 3, 512], F32, lstack)
            mt_ring = Ring(("mt", l), mt_t, 3)
            b1c = sb("b1c", [128, 32, 2], F32, lstack)
            shc, shk = shT[(l, "s2")], ("shT", l, "s2")

            plan = []
            for gi in range(len(groups)):
                plan += [("w1", gi, pw) for pw in range(8)] + [("w2", gi, pw) for pw in range(8)]
            issued = [0]
            slot = {}

            def ensure(n):
                while issued[0] < min(n + 1, len(plan)):
                    kind, gi, pw = plan[issued[0]]
                    if kind == "w1":
                        si, skey = W1_ring.next()
                        k.dma("sp", W1p[:, si, :, :],
                              w1b[l, :, pw * 512:(pw + 1) * 512].rearrange("(c p) n -> p c n", p=128),
                              reads=[("w1b", l, q) for q in range(4)], writes=[skey])
                    else:
                        si, skey = W2_ring.next()
                        k.dma("sp", W2p[:, si, :, :],
                              w2b[l, pw * 512:(pw + 1) * 512, :].rearrange("(c p) n -> p c n", p=128),
                              reads=[("w2b", l, pw)], writes=[skey])
                    slot[issued[0]] = (si, skey)
                    issued[0] += 1

            xtiles = {}

            def load_group(gi):
                tiles, v = groups[gi]
                for t in tiles:
                    xi, xkey = xm_ring.next()
                    k.dma("sp", xm_t[:, xi, :], xs_d[t, :, :], reads=[("xs", t)], writes=[xkey])
                    xtiles[(gi, t)] = (xi, xkey)

            def norm_pre(gi):
                tiles, v = groups[gi]
                S_ap, skey = S_of(v)
                out = []
                for t in tiles:
                    xi, xkey = xtiles[(gi, t)]
                    x_ap = xm_t[:, xi, :]
                    si, skey_stat = stat_ring.next()
                    ssq = stat[:, si, 0:1]
                    rstd = stat[:, si, 1:2]
                    k.op("act", lambda e: e.activation(out=junk[:], in_=x_ap, func=AF.Square, accum_out=ssq),
                         reads=[xkey], writes=["junk", skey_stat])
                    k.op("act", lambda e: e.activation(out=rstd, in_=ssq, func=AF.Ln, scale=1.0 / D, bias=epsc[:, 0:1]),
                         reads=[skey_stat, "epsc"], writes=[skey_stat])
                    k.op("act", lambda e: e.activation(out=rstd, in_=rstd, func=AF.Exp, scale=-0.5),
                         reads=[skey_stat], writes=[skey_stat])
                    qi, qkey = xsm_ring.next()
                    k.op("dve", lambda e: e.scalar_tensor_tensor(out=xsm_t[:, qi, :], in0=x_ap, scalar=rstd, in1=S_ap,
                                                                 op0=ALU.mult, op1=ALU.mult),
                         reads=[xkey, skey_stat, skey], writes=[qkey])
                    out.append((qi, qkey))
                return out

            def norm_T(gi, pre):
                hb = gi % 2
                for ti, (qi, qkey) in enumerate(pre):
                    b = nbank()
                    bv = banks[b][:].bitcast(BF16).rearrange("p (c t) -> p c t", c=8)
                    for c in range(8):
                        k.op("pe", lambda e, c=c: e.transpose(bv[:, c, :], xsm_t[:, qi, c * 128:(c + 1) * 128], identb[:]),
                             reads=[qkey, "identb"], writes=[bkey(b)], signal=(c == 7))
                    copy_op(evac_eng(), hTg[:, hb, :, ti * 128:(ti + 1) * 128], bv, [bkey(b)], [("hTg", l, hb, ti)])

            load_group(0)
            ensure(1)
            pre = norm_pre(0)
            norm_T(0, pre)
            pidx = 0
            for gi, (tiles, v) in enumerate(groups):
                nt = len(tiles)
                ntok = nt * 128
                hb = gi % 2
                hkeys = [("hTg", l, hb, ti) for ti in range(nt)]
                if gi + 1 < len(groups):
                    load_group(gi + 1)
                for pw in range(8):
                    ensure(pidx + 2)
                    si, skey = slot[pidx]
                    pidx += 1
                    if gi == 0:
                        bias_rows(shc, shk, lambda c, g: W1p[:, si, c, :], lambda g: [skey], 512,
                                  lambda g, row_ap, row_key: rows_to_cols(row_ap, row_key, 512, b1c, ("b1c", l, pw), pw * 4))
                    for ffc in range(4):
                        fg = pw * 4 + ffc
                        b = nbank()
                        for c in range(8):
                            k.op("pe", lambda e, c=c: e.matmul(banks[b][:, 0:ntok], W1p[:, si, c, ffc * 128:(ffc + 1) * 128],
                                                              hTg[:, hb, c, 0:ntok], start=(c == 0), stop=(c == 7)),
                                 reads=[skey] + hkeys, writes=[bkey(b)], signal=(c == 7))
                        ri, rkey = rt_ring.next()
                        k.op("act", lambda e: e.activation(out=rt_t[:, ri, 0:ntok], in_=banks[b][:, 0:ntok], func=AF.Relu,
                                                           bias=b1c[:, fg, v:v + 1]),
                             reads=[bkey(b), ("b1c", l, pw)], writes=[rkey])
                        k.op("act", lambda e: e.activation(out=h1T[:, fg, 0:ntok], in_=rt_t[:, ri, 0:ntok], func=AF.Square),
                             reads=[rkey], writes=[("h1T", l, fg)])
                if gi + 1 < len(groups):
                    pre = norm_pre(gi + 1)
                for pw in range(8):
                    ensure(pidx + 2)
                    si, skey = slot[pidx]
                    pidx += 1
                    for ti in range(nt):
                        for nh in range(2):
                            b = ti * 2 + nh
                            for ffc in range(4):
                                fg = pw * 4 + ffc
                                first = (pw == 0 and ffc == 0)
                                last = (pw == 7 and ffc == 3)
                                k.op("pe", lambda e, ffc=ffc, fg=fg, first=first, last=last:
                                     e.matmul(banks[b][:, :], h1T[:, fg, ti * 128:(ti + 1) * 128],
                                              W2p[:, si, ffc, nh * 512:(nh + 1) * 512], start=first, stop=last),
                                     reads=[skey, ("h1T", l, fg)], writes=[bkey(b)],
                                     signal=(last or (ti == nt - 1 and nh == 1 and ffc == 3)))
                G_ap, gkey = G_of(v)
                for ti, t in enumerate(tiles):
                    xi, xkey = xtiles[(gi, t)]
                    for nh in range(2):
                        b = ti * 2 + nh
                        mi, mkey = mt_ring.next()
                        k.op("dve", lambda e: e.tensor_tensor(out=mt_t[:, mi, :], in0=banks[b][:, :],
                                                              in1=G_ap[:, nh * 512:(nh + 1) * 512], op=ALU.mult),
                             reads=[bkey(b), gkey], writes=[mkey])
                        k.op("dve", lambda e: e.tensor_tensor(out=xm_t[:, xi, nh * 512:(nh + 1) * 512],
                                                              in0=xm_t[:, xi, nh * 512:(nh + 1) * 512],
                                                              in1=mt_t[:, mi, :], op=ALU.add),
                             reads=[mkey, xkey], writes=[xkey])
                    k.dma("sp", dst_fn(t), xm_t[:, xi, :], reads=[xkey], writes=[dkey_fn(t)])
                if gi + 1 < len(groups):
                    norm_T(gi + 1, pre)

        ML0 = es.enter_context(ExitStack())
        bc0b_t = sb("bc0b", [128, 4, D], F32, ML0)
        bc0b_names = {}

        def bc0b_alloc(n, v):
            kk_ = (n, v)
            if kk_ not in bc0b_names:
                bc0b_names[kk_] = len(bc0b_names)
            i = bc0b_names[kk_]
            BC[(0, n, v)] = (bc0b_t[:, i, :], ("bc0b", i))
            return BC[(0, n, v)]

        modulation(0, [3, 4, 5], [0, 1], bc0b_alloc)
        ADA1_KEYS = convert_bg(ada1_b, ada_w[1], "ada1_b", 128)
        WQKV_KEYS = convert_bg(wqkv_b, w_qkv, "wqkv_b", 256)
        WNAO_KEYS = convert_bg(wnao_b, w_nao, "wnao_b", 512)
        groups0 = [([4 * g + t for t in range(4)], 0) for g in range(5)] + [([20, 21], 1)]
        mlp_layer(0, groups0, lambda v: BC[(0, "c2", v)], lambda v: BC[(0, "g2", v)],
                  lambda t: xs_d[t, :, :], lambda t: ("xs", t), ML0)
        k.barrier()
        ML0.close()
        L0.close()
        dbg_dump("x1", xs_d[:, :, :], [NALL, 128, D], F32, XS_KEYS)
        if stop == "L0":
            k.finish()
            return nc, dbg_out
        L1 = es.enter_context(ExitStack())
        bc1_t = sb("bc1", [128, 5, D], F32, L1)
        bc1_names = {}

        def bc1_alloc(n, v):
            kk_ = (n, v)
            if kk_ not in bc1_names:
                bc1_names[kk_] = len(bc1_names)
            i = bc1_names[kk_]
            BC[(1, n, v)] = (bc1_t[:, i, :], ("bc1", i))
            return BC[(1, n, v)]

        modulation(1, [0, 1], [0, 1], bc1_alloc, src_bf16=ada1_b, src_keys=ADA1_KEYS)

        AT = es.enter_context(ExitStack())
        em_t = sb("em_t", [128, 12, 128], BF16, AT)
        k.dma("pool", em_t[:], EBM_d[:, :, :], writes=["em"])
        ebf_t = sb("ebf_t", [128, 4, 12, 128], F32, AT)
        ebf_ring = Ring("ebf", ebf_t, 4)
        ebb_t = sb("ebb_t", [128, 2, 12, 128], BF16, AT)
        ebb_ring = Ring("ebb", ebb_t, 2)
        EB_UNITS = [[0, 1, 2, 3], [4, 5, 6], [7, 8, 9], [10, 11, 12], [13, 14, 15]]
        eb_slots = {}

        def eb_load(u):
            if u >= len(EB_UNITS):
                return
            for h in EB_UNITS[u]:
                fi, fkey = ebf_ring.next()
                k.dma("act", ebf_t[:, fi, :, :], EBB_d[:, h, :, :], writes=[fkey])
                eb_slots[h] = (fi, fkey)

        def eb_unit(u):
            if u >= len(EB_UNITS):
                return
            for h in EB_UNITS[u]:
                fi, fkey = eb_slots[h]
                bi, bkey_ = ebb_ring.next()
                k.op("act", lambda e: e.activation(out=ebb_t[:, bi, :, :], in_=ebf_t[:, fi, :, :], func=AF.Exp),
                     reads=[fkey], writes=[bkey_])
                k.op("pool", lambda e: e.tensor_tensor(out=ebb_t[:, bi, :, :], in0=ebb_t[:, bi, :, :], in1=em_t[:, :, :],
                                                       op=ALU.mult), reads=[bkey_, "em"], writes=[bkey_])
                k.dma("pool", EBd[h // 2, :, h % 2, :, :], ebb_t[:, bi, :, :], reads=[bkey_], writes=[("EBd", h)])

        QK = es.enter_context(ExitStack())
        wq_t = sb("wq_t", [128, 8, 3 * D], BF16, QK)
        for cb in (1, 0, 2):
            for r0 in range(0, D, 512):
                c0, c1 = r0 // 128, (r0 + 512) // 128
                k.dma("sp", wq_t[:, c0:c1, cb * D:(cb + 1) * D],
                      wqkv_b[r0:r0 + 512, cb * D:(cb + 1) * D].rearrange("(c p) n -> p c n", p=128),
                      reads=WQKV_KEYS, writes=[("wq", cb, c) for c in range(c0, c1)])
        WQ_KEYS = [("wq", cb, c) for cb in range(3) for c in range(8)]
        convert_mlp(1)
        bq_cols = sb("bq_cols", [128, 16, 2], F32, QK)
        bv_row = sb("bv_row", [2, D], BF16, QK)
        qkg_t = sb("qkg_t", [128, 2], F32, QK)
        blk1 = sb("blk1", [128, 128], BF16, QK)
        eps64 = sb("eps64", [128, 1], F32, QK)
        k.dma("sp", qkg_t[:], qkg[:, :], writes=["qkg"])
        k.op("dve", lambda e: e.tensor_scalar(qkg_t[:, 0:1], qkg_t[:, 0:1], 0.125, None, op0=ALU.mult),
             reads=["qkg"], writes=["qkg"])
        k.op("pool", lambda e: e.memset(blk1[:], 0.0), writes=["blk1"])
        k.op("pool", lambda e: e.memset(blk1[0:64, 0:64], 1.0), writes=["blk1"])
        k.op("pool", lambda e: e.memset(blk1[64:128, 64:128], 1.0), writes=["blk1"])
        k.op("dve", lambda e: e.memset(eps64[:], EPS), writes=["eps64"])

        def _bq_out(g, row_ap, row_key):
            if g < 4:
                rows_to_cols(row_ap, row_key, 512, bq_cols, "bq_cols", g * 4)
            else:
                k.op("act", lambda e: e.copy(out=bv_row[:, (g - 4) * 512:(g - 3) * 512], in_=row_ap),
                     reads=[row_key], writes=[("bv_row", g - 4)])

        bias_rows(shT[(1, "s1")], ("shT", 1, "s1"), lambda c, g: wq_t[:, c, g * 512:(g + 1) * 512],
                  lambda g: WQ_KEYS, 3 * D, _bq_out)

        xq_t = sb("xq_t", [128, 6, D], F32, QK)
        xq_ring = Ring("xq", xq_t, 6)
        hTq = sb("hTq", [128, 2, 8, 512], BF16, QK)
        kt_t = sb("kt_t", [128, 3, 512], F32, QK)
        kt_ring = Ring("kt", kt_t, 3)
        sq_t = sb("sq_t", [128, 3, 512], BF16, QK)
        sq_ring = Ring("sq", sq_t, 3)
        rs_t = sb("rs_t", [128, 2, 512], F32, QK)
        rs_ring = Ring("rs", rs_t, 2)
        kn_t = sb("kn_t", [128, 4, 512], BF16, QK)
        kn_ring = Ring("kn", kn_t, 4)
        vp_t = sb("vp_t", [128, 3, 16, 65], BF16, QK)
        vp_ring = Ring("vp", vp_t, 3)
        for vi in range(3):
            k.op("pool", lambda e, vi=vi: e.memset(vp_t[:, vi, :, 64:65], 1.0), writes=[("vp1", vi)])

        qgroups = [([4 * g + t for t in range(4)], 0) for g in range(5)] + [([20, 21], 1)]
        QRANGE = {0: (256, 512), 1: (0, 512), 2: (0, 512), 3: (0, 512), 4: (0, 256)}

        def qk_s0(it, st):
            hb, lo, hi, wcol0, bcol, v, gcol, dst_dram, dkey = it
            wd = hi - lo
            b = nbank()
            for c in range(8):
                k.op("pe", lambda e, c=c: e.matmul(banks[b][:, 0:wd], wq_t[:, c, wcol0:wcol0 + 128], hTq[:, hb, c, lo:hi],
                                                  start=(c == 0), stop=(c == 7)),
                     reads=[("wq", wcol0 // D, c)] + [("hTq", hb, ti) for ti in range(4)], writes=[bkey(b)], signal=(c == 7))
            ki, kkey = kt_ring.next()
            si, skey = sq_ring.next()
            k.op("act", lambda e: e.activation(out=sq_t[:, si, 0:wd], in_=banks[b][:, 0:wd], func=AF.Square,
                                               bias=bq_cols[:, bcol, v:v + 1]),
                 reads=[bkey(b), "bq_cols"], writes=[skey])
            k.op("dve", lambda e: e.tensor_scalar(kt_t[:, ki, 0:wd], banks[b][:, 0:wd], bq_cols[:, bcol, v:v + 1], None,
                                                  op0=ALU.add),
                 reads=[bkey(b), "bq_cols", skey], writes=[kkey])
            st.update(ki=ki, kkey=kkey, si=si, skey=skey)

        def qk_s1(it, st):
            hb, lo, hi, wcol0, bcol, v, gcol, dst_dram, dkey = it
            wd = hi - lo
            ki, kkey, si, skey = st["ki"], st["kkey"], st["si"], st["skey"]
            b2 = nbank()
            k.op("pe", lambda e: e.matmul(banks[b2][:, 0:wd], blk1[:, :], sq_t[:, si, 0:wd], start=True, stop=True),
                 reads=["blk1", skey], writes=[bkey(b2)])
            ri, rkey = rs_ring.next()
            k.op("act", lambda e: e.activation(out=rs_t[:, ri, 0:wd], in_=banks[b2][:, 0:wd], func=AF.Ln,
                                               scale=1.0 / 64, bias=eps64[:, 0:1]),
                 reads=[bkey(b2), "eps64"], writes=[rkey])
            k.op("act", lambda e: e.activation(out=rs_t[:, ri, 0:wd], in_=rs_t[:, ri, 0:wd], func=AF.Exp, scale=-0.5),
                 reads=[rkey], writes=[rkey])
            ni, nkey = kn_ring.next()
            k.op("dve", lambda e: e.scalar_tensor_tensor(out=kn_t[:, ni, 0:wd], in0=kt_t[:, ki, 0:wd], scalar=gcol,
                                                         in1=rs_t[:, ri, 0:wd], op0=ALU.mult, op1=ALU.mult),
                 reads=[kkey, rkey, "qkg"], writes=[nkey])
            k.dma("sp", dst_dram, kn_t[:, ni, 0:wd], reads=[nkey], writes=[dkey])

        S1x1, S1x1k = BC[(1, "c1", 0)]
        S1c1, S1c1k = BC[(1, "c1", 1)]
        xq_tiles = {}

        def q_load(gi):
            tiles, v = qgroups[gi]
            for t in tiles:
                xi, xkey = xq_ring.next()
                k.dma("sp", xq_t[:, xi, :], xs_d[t, :, :], reads=[("xs", t)], writes=[xkey])
                xq_tiles[(gi, t)] = (xi, xkey)

        def q_norm_pre(gi):
            tiles, v = qgroups[gi]
            S_ap, skey_ = (S1x1, S1x1k) if v == 0 else (S1c1, S1c1k)
            out = []
            for t in tiles:
                xi, xkey = xq_tiles[(gi, t)]
                x_ap = xq_t[:, xi, :]
                si, skey_stat = stat_ring.next()
                ssq = stat[:, si, 0:1]
                rstd = stat[:, si, 1:2]
                k.op("act", lambda e: e.activation(out=junk[:], in_=x_ap, func=AF.Square, accum_out=ssq),
                     reads=[xkey], writes=["junk", skey_stat])
                k.op("act", lambda e: e.activation(out=rstd, in_=ssq, func=AF.Ln, scale=1.0 / D, bias=epsc[:, 0:1]),
                     reads=[skey_stat, "epsc"], writes=[skey_stat])
                k.op("act", lambda e: e.activation(out=rstd, in_=rstd, func=AF.Exp, scale=-0.5),
                     reads=[skey_stat], writes=[skey_stat])
                qi, qkey = xs_ring.next()
                k.op("dve", lambda e: e.scalar_tensor_tensor(out=xs_t[:, qi, :], in0=x_ap, scalar=rstd, in1=S_ap,
                                                             op0=ALU.mult, op1=ALU.mult),
                     reads=[xkey, skey_stat, skey_], writes=[qkey])
                out.append((qi, qkey))
            return out

        def q_norm_T(gi, pre):
            hb = gi % 2
            for ti, (qi, qkey) in enumerate(pre):
                b = nbank()
                bv = banks[b][:].bitcast(BF16).rearrange("p (c t) -> p c t", c=8)
                for c in range(8):
                    k.op("pe", lambda e, c=c: e.transpose(bv[:, c, :], xs_t[:, qi, c * 128:(c + 1) * 128], identb[:]),
                         reads=[qkey, "identb"], writes=[bkey(b)], signal=(c == 7))
                copy_op(evac_eng(), hTq[:, hb, :, ti * 128:(ti + 1) * 128], bv, [bkey(b)], [("hTq", hb, ti)])

        eb_load(0)
        q_load(0)
        pre = q_norm_pre(0)
        q_norm_T(0, pre)
        for gi, (tiles, v) in enumerate(qgroups):
            hb = gi % 2
            ntok = len(tiles) * 128
            tok0 = (tiles[0]) * 128
            if gi + 1 < len(qgroups):
                q_load(gi + 1)
            its = []
            for kc in range(8):
                its.append((hb, 0, ntok, D + kc * 128, 8 + kc, v, qkg_t[:, 1:2], KTd[kc, :, tok0:tok0 + ntok], ("KTd", kc, gi)))
            if gi in QRANGE:
                lo, hi = QRANGE[gi]
                q0 = tok0 + lo - 256
                for qc in range(8):
                    its.append((hb, lo, hi, qc * 128, qc, v, qkg_t[:, 0:1], QTd[qc, :, q0:q0 + (hi - lo)], ("QTd", qc, gi)))
            run_pipeline(len(its), [lambda i_, st_: qk_s0(its[i_], st_), lambda i_, st_: qk_s1(its[i_], st_)], [0, 1])
            eb_unit(gi)
            eb_load(gi + 1)
            if gi + 1 < len(qgroups):
                pre = q_norm_pre(gi + 1)
            for ti, t in enumerate(tiles):
                vi, vkey = vp_ring.next()
                for nh in range(2):
                    b = nbank()
                    for c in range(8):
                        k.op("pe", lambda e, c=c: e.matmul(banks[b][:, :], hTq[:, hb, c, ti * 128:(ti + 1) * 128],
                                                          wq_t[:, c, 2 * D + nh * 512:2 * D + (nh + 1) * 512],
                                                          start=(c == 0), stop=False),
                             reads=[("wq", 2, c), ("hTq", hb, ti)], writes=[bkey(b)], signal=False)
                    k.op("pe", lambda e: e.matmul(banks[b][:, :], sel2b[:, v, :], bv_row[:, nh * 512:(nh + 1) * 512],
                                                  start=False, stop=True),
                         reads=["sel2b", ("bv_row", nh)], writes=[bkey(b)])
                    copy_op(evac_eng(), vp_t[:, vi, nh * 8:(nh + 1) * 8, 0:64],
                            banks[b][:, :].rearrange("p (h d) -> p h d", d=64), [bkey(b), ("vp1", vi)], [(vkey, nh)])
                k.dma("sp", Vd[:, t, :, :].rearrange("h k c -> k h c"),
                      vp_t[:, vi, :, :].rearrange("p (hp two) d -> p hp (two d)", two=2),
                      reads=[(vkey, 0), (vkey, 1)], writes=[("Vd", t)])
            if gi + 1 < len(qgroups):
                q_norm_T(gi + 1, pre)
        k.barrier()
        QK.close()
        AT.close()
        KTD_KEYS = lambda hp: [("KTd", hp, gi) for gi in range(6)]
        QTD_KEYS = lambda hp: [("QTd", hp, gi) for gi in range(5)]
        VD_KEYS = [("Vd", t) for t in range(NALL)]
        dbg_dump("QTd", QTd[:, :, :], [8, 128, 2048], BF16, [kk_ for hp in range(8) for kk_ in QTD_KEYS(hp)])
        dbg_dump("KTd", KTd[:, :, :], [8, 128, NALL * 128], BF16, [kk_ for hp in range(8) for kk_ in KTD_KEYS(hp)])
        dbg_dump("Vd", Vd[:, :, :, :], [8, NALL, 128, 130], BF16, VD_KEYS)
        if stop == "QKV":
            k.finish()
            return nc, dbg_out

        ATT = es.enter_context(ExitStack())
        attn_tok = sb("attn_tok", [128, 16, D], BF16, ATT)
        ATN = es.enter_context(ExitStack())
        rm_t = sb("rm_t", [128, 4, 6, 128], BF16, ATN)
        k.dma("pool", rm_t[:], RM_d[:, :, :, :], writes=["rm"])
        qt_t = sb("qt_t", [128, 2, 2048], BF16, ATN)
        ktt_t = sb("ktt_t", [128, 2, NALL * 128], BF16, ATN)
        vv_t = sb("vv_t", [128, 2, NALL, 130], BF16, ATN)
        eb_t = sb("eb_t", [128, 2, 2, 12, 128], BF16, ATN)
        pt_t = sb("pt_t", [128, 4, 1024], BF16, ATN)
        pt_ring = Ring("pt", pt_t, 4)
        rec_t = sb("rec_t", [128, 4, 2], F32, ATN)
        rec_ring = Ring("rec", rec_t, 4)
        CLS = {0: 0, 1: 1, 14: 2, 15: 3}

        def load_pair(hp):
            s = hp % 2
            k.dma("sp", qt_t[:, s, :], QTd[hp, :, :], reads=QTD_KEYS(hp), writes=[("qt", s)])
            k.dma("sp", ktt_t[:, s, :], KTd[hp, :, :], reads=KTD_KEYS(hp), writes=[("ktt", s)])
            k.dma("sp", vv_t[:, s, :, :], Vd[hp, :, :, :].rearrange("t k c -> k t c"), reads=VD_KEYS, writes=[("vv", s)])
            k.dma("sp", eb_t[:, s, :, :, :], EBd[hp, :, :, :, :], reads=[("EBd", 2 * hp), ("EBd", 2 * hp + 1)],
                  writes=[("eb", s)])

        load_pair(0)
        items = [(hp, qb, hh) for hp in range(8) for qb in range(16) for hh in range(2)]

        def tile_list(qb):
            if qb == 0:
                return list(range(0, 6)), 1
            if qb == 15:
                return list(range(14, 20)), 0
            if qb in (1, 14):
                return list(range(qb, qb + 5)), 1
            return list(range(qb, qb + 5)), 7

        obank = {}

        def a_s0(i, st):
            hp, qb, hh = items[i]
            s_ = hp % 2
            tl, s0 = tile_list(qb)
            tiles_all = tl + [20, 21]
            nt = len(tiles_all)
            pr = slice(hh * 64, (hh + 1) * 64)
            bA, bB = nbank(), nbank()
            for si_, t in enumerate(tiles_all):
                bb_ = bA if si_ < 4 else bB
                col = (si_ % 4) * 128
                last = (si_ == 3 or si_ == nt - 1)
                k.op("pe", lambda e, t=t, bb_=bb_, col=col: e.matmul(
                    banks[bb_][:, col:col + 128], ktt_t[pr, s_, t * 128:(t + 1) * 128],
                    qt_t[pr, s_, qb * 128:(qb + 1) * 128], start=True, stop=True),
                    reads=[("ktt", s_), ("qt", s_)], writes=[bkey(bb_)], signal=last)
            st.update(bA=bA, bB=bB, tiles_all=tiles_all, nt=nt, nloc=len(tl), s0=s0)

        def a_s1(i, st):
            hp, qb, hh = items[i]
            s_ = hp % 2
            bA, bB, nt, nloc, s0 = st["bA"], st["bB"], st["nt"], st["nloc"], st["s0"]
            pi, pkey = pt_ring.next()
            st.update(pi=pi, pkey=pkey)
            k.op("act", lambda e: e.activation(out=pt_t[:, pi, 0:512], in_=banks[bA][:, :], func=AF.Exp),
                 reads=[bkey(bA)], writes=[(pkey, 0)])
            nB = (nt - 4) * 128
            k.op("act", lambda e: e.activation(out=pt_t[:, pi, 512:512 + nB], in_=banks[bB][:, 0:nB], func=AF.Exp),
                 reads=[bkey(bB)], writes=[(pkey, 1)])
            k.op("dve", lambda e: e.tensor_tensor(
                out=pt_t[:, pi, 0:nloc * 128], in0=pt_t[:, pi, 0:nloc * 128],
                in1=eb_t[:, s_, hh, s0:s0 + nloc, :].rearrange("p a b -> p (a b)"), op=ALU.mult),
                reads=[(pkey, 0), (pkey, 1), ("eb", s_)], writes=[(pkey, 0), (pkey, 1)])
            if qb in CLS:
                k.op("pool", lambda e: e.tensor_tensor(
                    out=pt_t[:, pi, 0:nloc * 128], in0=pt_t[:, pi, 0:nloc * 128],
                    in1=rm_t[:, CLS[qb], 0:nloc, :].rearrange("p a b -> p (a b)"), op=ALU.mult),
                    reads=[(pkey, 0), (pkey, 1), "rm"], writes=[(pkey, 0), (pkey, 1)])

        def a_s2(i, st):
            hp, qb, hh = items[i]
            s_ = hp % 2
            pi, pkey, tiles_all, nt = st["pi"], st["pkey"], st["tiles_all"], st["nt"]
            if hh == 0:
                obank[(hp, qb)] = nbank()
            bO = obank[(hp, qb)]
            for si_, t in enumerate(tiles_all):
                k.op("pe", lambda e, t=t, si_=si_: e.matmul(
                    banks[bO][:, hh * 65:(hh + 1) * 65], pt_t[:, pi, si_ * 128:(si_ + 1) * 128],
                    vv_t[:, s_, t, hh * 65:(hh + 1) * 65], start=(si_ == 0), stop=(si_ == nt - 1)),
                    reads=[(pkey, 0), (pkey, 1), ("vv", s_)], writes=[bkey(bO)], signal=(si_ == nt - 1))
            if hh == 1:
                ri, rkey = rec_ring.next()
                Ov = banks[bO][:, 0:130].rearrange("p (h c) -> p h c", c=65)
                k.op("dve", lambda e: e.reciprocal(out=rec_t[:, ri, :], in_=Ov[:, :, 64]), reads=[bkey(bO)], writes=[rkey])
                k.op("dve", lambda e: e.tensor_tensor(
                    out=attn_tok[:, qb, hp * 128:(hp + 1) * 128].rearrange("p (h d) -> p h d", d=64),
                    in0=Ov[:, :, 0:64], in1=rec_t[:, ri, :].unsqueeze(2).to_broadcast([128, 2, 64]), op=ALU.mult),
                    reads=[bkey(bO), rkey], writes=[("attn", qb, hp)])

        for hp_ in range(8):
            if hp_ + 1 < 8:
                load_pair(hp_ + 1)
            run_pipeline(32, [lambda i_, st_: a_s0(hp_ * 32 + i_, st_), lambda i_, st_: a_s1(hp_ * 32 + i_, st_),
                              lambda i_, st_: a_s2(hp_ * 32 + i_, st_)], [0, 1, 2])
            if hp_ == 0:
                modulation(1, [2, 3, 4, 5], [0], bc1_alloc, src_bf16=ada1_b, src_keys=ADA1_KEYS)
        k.barrier()
        ATN.close()
        dbg_dump("attn", attn_tok[:, :, :], [128, 16, D], BF16, [("attn", qb, hp) for qb in range(16) for hp in range(8)])

        AO = es.enter_context(ExitStack())
        Wo2 = sb("Wo2", [128, 8, D], BF16, AO)
        load_w_bf16(Wo2, wnao_b, "Wo2", WNAO_KEYS)
        aT_t = sb("aT_t", [128, 2, 8, 128], BF16, AO)
        aT_ring = Ring("aT", aT_t, 2)
        xa_t = sb("xa_t", [128, 4, D], F32, AO)
        xa_ring = Ring("xa", xa_t, 4)
        ga_t = sb("ga_t", [128, 2, 512], F32, AO)
        ga_ring = Ring("ga", ga_t, 2)
        G1x1, G1x1k = BC[(1, "g1", 0)]
        aslots = {}

        def a_load(qb):
            xi, xkey = xa_ring.next()
            k.dma("sp", xa_t[:, xi, :], xs_d[qb + 2, :, :], reads=[("xs", qb + 2)], writes=[xkey])
            aslots[qb] = (xi, xkey)

        a_load(0)
        a_load(1)
        for qb in range(16):
            t = qb + 2
            if qb + 2 < 16:
                a_load(qb + 2)
            xi, xkey = aslots[qb]
            b = nbank()
            bv = banks[b][:].bitcast(BF16).rearrange("p (c t) -> p c t", c=8)
            for c in range(8):
                k.op("pe", lambda e, c=c: e.transpose(bv[:, c, :], attn_tok[:, qb, c * 128:(c + 1) * 128], identb[:]),
                     reads=[("attn", qb, c), "identb"], writes=[bkey(b)], signal=(c == 7))
            ai, akey = aT_ring.next()
            copy_op(evac_eng(), aT_t[:, ai, :, :], bv, [bkey(b)], [akey])
            for nh in range(2):
                b2 = nbank()
                for kc in range(8):
                    k.op("pe", lambda e, kc=kc: e.matmul(banks[b2][:, :], aT_t[:, ai, kc, :],
                                                        Wo2[:, kc, nh * 512:(nh + 1) * 512], start=(kc == 0), stop=(kc == 7)),
                         reads=[akey, ("Wo2", kc)], writes=[bkey(b2)], signal=(kc == 7))
                gi_, gkey_ = ga_ring.next()
                k.op("dve", lambda e: e.tensor_tensor(out=ga_t[:, gi_, :], in0=banks[b2][:, :],
                                                      in1=G1x1[:, nh * 512:(nh + 1) * 512], op=ALU.mult),
                     reads=[bkey(b2), G1x1k], writes=[gkey_])
                k.op("pool", lambda e: e.tensor_tensor(out=xa_t[:, xi, nh * 512:(nh + 1) * 512],
                                                       in0=xa_t[:, xi, nh * 512:(nh + 1) * 512], in1=ga_t[:, gi_, :], op=ALU.add),
                     reads=[gkey_, xkey], writes=[xkey])
            k.dma("act", xs_d[t, :, :], xa_t[:, xi, :], reads=[xkey], writes=[("xs", t)])
        k.barrier()
        AO.close()
        ATT.close()
        dbg_dump("x2m", xs_d[2:18, :, :], [16, 128, D], F32, [("xs", t) for t in range(2, 18)])
        if stop == "ATT":
            k.finish()
            return nc, dbg_out

        ML1 = es.enter_context(ExitStack())
        groups1 = [([2 + 4 * g + t for t in range(4)], 0) for g in range(4)]
        mlp_layer(1, groups1, lambda v: BC[(1, "c2", v)], lambda v: BC[(1, "g2", v)],
                  lambda t: y[(t - 2) * 128:(t - 1) * 128, :], lambda t: ("y", t), ML1)
        k.finish()
        print("ops", k.n_ops, "waits", k.n_waits)
    return nc, dbg_out


def host_consts(jq):
    rot = 32 * jq - 4
    grow = (rot + np.arange(128)) % 128
    t2 = np.arange(64)
    k1 = np.arange(128)
    tok = 64 * grow[:, None] + t2[None, :]
    ang = 2 * np.pi * ((tok[:, :, None] * k1[None, None, :]) % 8192) / 8192.0
    MT = np.stack([np.cos(ang), -np.sin(ang)], axis=2).astype(np.float32)
    k2 = (16 * jq - 2 + np.arange(20)) % 64
    a = 2 * np.pi * ((t2[:, None] * k2[None, :]) % 64) / 64.0
    s = 1.0 / np.sqrt(8192.0)
    WC = np.zeros((128, 2, 20))
    WC[0:64, 0] = np.cos(a) * s
    WC[0:64, 1] = -np.sin(a) * s
    WC[64:128, 0] = np.sin(a) * s
    WC[64:128, 1] = np.cos(a) * s
    c = np.arange(64)
    b = 2 * np.pi * ((c[:, None] * c[None, :]) % 64) / 64.0
    BCS = np.zeros((128, 2, 128))
    for gl in range(2):
        BCS[gl * 64:(gl + 1) * 64, 0, gl * 64:(gl + 1) * 64] = np.cos(b) / 8.0
        BCS[gl * 64:(gl + 1) * 64, 1, gl * 64:(gl + 1) * 64] = np.sin(b) / 8.0
    t = (np.arange(2)[None, :] * 128 + np.arange(128)[:, None])
    kk = np.arange(256)
    a2 = 2 * np.pi * ((t[:, :, None] * kk[None, None, :]) % 256) / 256.0
    CT = np.stack([np.cos(a2), -np.sin(a2)], axis=2) / 16.0
    gmask = np.ones((128, 2))
    if jq == 0:
        gmask[:, 0] = 0.0
    if jq == 3:
        gmask[:, 1] = 0.0
    return dict(MT=MT, WC=WC.astype(np.float32), BCS=BCS.astype(np.float32), CT256=CT.astype(np.float32),
                gmask=gmask.astype(np.float32))


def make_in_maps(inputs):
    x = np.asarray(inputs["x"], np.float32)
    c = np.asarray(inputs["c"], np.float32)
    ctx = np.asarray(inputs["ctx"], np.float32)
    c_ctx = np.asarray(inputs["c_ctx"], np.float32)
    shared = dict(
        ada_w=np.ascontiguousarray(inputs["ada_w"], np.float32),
        ada_b=np.ascontiguousarray(inputs["ada_b"], np.float32),
        nmix=np.ascontiguousarray(inputs["norm_mix_g"], np.float32),
        nmlp=np.ascontiguousarray(inputs["norm_mlp_g"], np.float32),
        w1=np.ascontiguousarray(inputs["mlp_w1"], np.float32),
        w2=np.ascontiguousarray(inputs["mlp_w2"], np.float32),
        w_in=np.ascontiguousarray(inputs["ab_w_in"][0], np.float32),
        convw=np.ascontiguousarray(np.asarray(inputs["ab_conv_w"][0], np.float32).reshape(3, 4, 128).transpose(2, 1, 0)),
        w_abo=np.ascontiguousarray(inputs["ab_w_out"][0], np.float32),
        w_qkv=np.ascontiguousarray(inputs["na_w_qkv"][0], np.float32),
        qkg=np.ascontiguousarray(np.stack([np.tile(np.asarray(inputs["na_q_g"][0], np.float32), 2),
                                           np.tile(np.asarray(inputs["na_k_g"][0], np.float32), 2)], axis=1)),
        w_nao=np.ascontiguousarray(inputs["na_w_out"][0], np.float32),
        ident=np.eye(128, dtype=np.float32),
        sel=np.stack([np.stack([np.ones(128), np.zeros(128)]),
                      np.stack([np.zeros(128), np.ones(128)])]).astype(np.float32),
    )
    maps = []
    for j in range(NCORES):
        b, jq = j // 4, j % 4
        rot = 32 * jq - 4
        xb = x[b].reshape(128, 64, D)
        m = dict(shared)
        m["xf"] = np.ascontiguousarray(np.roll(xb, -rot, axis=0))
        m["ctxb"] = np.ascontiguousarray(ctx[b])
        cond = np.stack([c[b], c_ctx], axis=1)
        m["condT"] = np.ascontiguousarray(cond.reshape(8, 128, 2).transpose(1, 0, 2))
        m.update(host_consts(jq))
        m.update(attn_tables(inputs, jq))
        maps.append(m)
    return maps


def attn_tables(inputs, jq):
    rpb = np.asarray(inputs["na_rpb"][0], np.float32)
    R0 = 32 * jq
    kr = np.arange(128) // 64
    kc = np.arange(128) % 64
    qr = np.arange(128) // 64
    qc = np.arange(128) % 64
    cs = np.clip(qc - 8, 0, 48)
    colok = (kc[:, None] >= cs[None, :]) & (kc[:, None] < cs[None, :] + 16)
    dc = np.clip(kc[:, None] - qc[None, :] + 15, 0, 30)
    slots = [(-3, False), (-2, False), (-1, False), (0, False), (1, False), (2, False), (3, False),
             (-2, True), (-1, True), (0, True), (1, True), (2, True)]
    ebias = np.zeros((128, 16, 12, 128), np.float32)
    emask = np.zeros((128, 12, 128), np.float32)
    for si, (dl, interior) in enumerate(slots):
        dr = 2 * dl + kr[:, None] - qr[None, :]
        ok = colok & (np.abs(dr) <= 7)
        if interior:
            ok = ok & (dr >= -4) & (dr <= 3)
        ro = np.clip(dr + 7, 0, 14)
        g = rpb[:, ro, dc]
        ebias[:, :, si, :] = np.where(ok[None], g, np.float32(0)).transpose(1, 0, 2)
        emask[:, si, :] = ok
    rmask = np.zeros((128, 4, 6, 128), np.float32)
    for cls, qb in enumerate((0, 1, 14, 15)):
        if qb == 0:
            tl = list(range(0, 6))
        elif qb == 15:
            tl = list(range(14, 20))
        else:
            tl = list(range(qb, qb + 5))
        r = R0 + 2 * qb + qr
        rs = np.clip(r - 4, 0, 120)
        for idx, t in enumerate(tl):
            gk = R0 - 4 + 2 * t + kr
            ok = (gk[:, None] >= 0) & (gk[:, None] <= 127) & (gk[:, None] >= rs[None, :]) & (gk[:, None] < rs[None, :] + 8)
            rmask[:, cls, idx, :] = ok
    return dict(ebias=ebias, emask=emask, rmask=rmask)


def kernel(**inputs):
    nc, _ = build()
    maps = make_in_maps(inputs)
    res = run_bass_kernel_spmd(nc, maps, core_ids=list(range(NCORES)))
    out = np.zeros((2, 8192, D), np.float32)
    for j in range(NCORES):
        b, jq = j // 4, j % 4
        out[b, jq * 2048:(jq + 1) * 2048] = res.results[j]["y"]
    return out
```

```python
import numpy as np
from contextlib import ExitStack
import concourse.bass as bass
import concourse.mybir as mybir
from concourse.bass_utils import run_bass_kernel_spmd

F32 = mybir.dt.float32
BF16 = mybir.dt.bfloat16
AF = mybir.ActivationFunctionType
ALU = mybir.AluOpType
AX = mybir.AxisListType

NCORES = 8
D = 1024
NEXT = 20
NCTX = 2
NALL = 22
NTE = NEXT * 128
EPS = 1e-6


class Tok:
    __slots__ = ("key", "count", "clock")

    def __init__(self, key, count, clock):
        self.key = key
        self.count = count
        self.clock = clock


class K:
    def __init__(self, nc, es, n_dma_sems=48):
        self.nc = nc
        self.eng = {"pe": nc.tensor, "act": nc.scalar, "dve": nc.vector,
                    "pool": nc.gpsimd, "sp": nc.sync}
        self.sem, self.cnt, self.obs = {}, {}, {}
        for n in self.eng:
            self.sem[n] = es.enter_context(nc.semaphore("s_" + n))
            self.cnt[n] = 0
            self.obs[n] = {}
        self.dma_sems = {}
        self.dma_rr = {}
        self.dma_last = {}
        for q, n in (("sp", 32), ("act", 10), ("pool", 16), ("bg", 16)):
            self.dma_sems[q] = []
            self.dma_rr[q] = 0
            for i in range(n):
                kk = ("dma", q, i)
                self.sem[kk] = es.enter_context(nc.semaphore("s_dma_%s%d" % (q, i)))
                self.cnt[kk] = 0
                self.dma_sems[q].append(kk)
                self.dma_last[kk] = None
        self.res = {}
        self.pe_pending = []
        self.n_ops = 0
        self.n_waits = 0

    def _deps(self, reads, writes):
        raw, other = [], []
        for r in reads:
            e = self.res.get(r)
            if e is not None and e[0] is not None:
                raw.append(e[0])
        for w in writes:
            e = self.res.get(w)
            if e is not None:
                if e[0] is not None:
                    other.append(e[0])
                other.extend(e[1])
        return raw, other

    def _wait(self, en, tok):
        assert tok.count is not None, "dependency on unsignaled PE op"
        ob = self.obs[en]
        if ob.get(tok.key, 0) >= tok.count:
            return
        self.eng[en].wait_ge(self.sem[tok.key], tok.count)
        self.n_waits += 1
        for kk, v in tok.clock.items():
            if ob.get(kk, 0) < v:
                ob[kk] = v
        ob[tok.key] = tok.count

    def _record(self, tok, reads, writes):
        for r in reads:
            e = self.res.setdefault(r, [None, []])
            e[1].append(tok)
        for w in writes:
            self.res[w] = [tok, []]

    def op(self, en, fn, reads=(), writes=(), signal=True):
        raw, other = self._deps(reads, writes)
        for t in raw:
            if t.key == en and en == "pe":
                continue
            self._wait(en, t)
        for t in other:
            if t.key == en:
                continue
            self._wait(en, t)
        ins = fn(self.eng[en])
        self.n_ops += 1
        if signal:
            self.cnt[en] += 1
            ins.then_inc(self.sem[en], 1)
            tok = Tok(en, self.cnt[en], dict(self.obs[en]))
            if en == "pe":
                for p in self.pe_pending:
                    p.count = tok.count
                    p.clock = tok.clock
                self.pe_pending = []
        else:
            assert en == "pe"
            tok = Tok(en, None, None)
            self.pe_pending.append(tok)
        self._record(tok, reads, writes)
        return tok

    def dma(self, q, out, in_, reads=(), writes=(), bg=False, **kw):
        raw, other = self._deps(reads, writes)
        for t in raw + other:
            self._wait(q, t)
        pq = "bg" if bg else q
        kk = self.dma_sems[pq][self.dma_rr[pq]]
        self.dma_rr[pq] = (self.dma_rr[pq] + 1) % len(self.dma_sems[pq])
        prev = self.dma_last[kk]
        if prev is not None:
            self._wait(q, prev)
        self.cnt[kk] += 16
        self.eng[q].dma_start(out=out, in_=in_, **kw).then_inc(self.sem[kk], 16)
        self.n_ops += 1
        tok = Tok(kk, self.cnt[kk], dict(self.obs[q]))
        self.dma_last[kk] = tok
        self._record(tok, reads, writes)
        return tok

    def barrier(self):
        assert not self.pe_pending
        toks = []
        for en in self.eng:
            if self.cnt[en] > 0:
                toks.append(Tok(en, self.cnt[en], {}))
        for q in ("sp", "act", "pool"):
            for kk in self.dma_sems[q]:
                if self.dma_last[kk] is not None:
                    toks.append(self.dma_last[kk])
        for en in self.eng:
            for t in toks:
                if t.key != en:
                    self._wait(en, t)

    def finish(self, en="sp"):
        for q in self.dma_sems:
            for kk in self.dma_sems[q]:
                t = self.dma_last[kk]
                if t is not None:
                    self._wait(en, t)


def run_pipeline(n, stages, skews):
    st = [dict() for _ in range(n)]
    for s_ in range(n + max(skews)):
        for f, sk in zip(stages, skews):
            it = s_ - sk
            if 0 <= it < n:
                f(it, st[it])


class Ring:
    def __init__(self, name, t, n):
        self.name, self.t, self.n, self.i = name, t, n, 0

    def next(self):
        i = self.i
        self.i = (i + 1) % self.n
        return i, (self.name, i)


def build(dbg=None, stop=None):
    dbg = dbg or set()
    nc = bass.Bass("TRN2", target_bir_lowering=False)

    def din(name, shape, dt=F32):
        return nc.dram_tensor(name, list(shape), dt, kind="ExternalInput").ap()

    def dscr(name, shape, dt=BF16):
        return nc.dram_tensor(name, list(shape), dt, kind="Internal").ap()

    def dout(name, shape, dt=F32):
        return nc.dram_tensor(name, list(shape), dt, kind="ExternalOutput").ap()

    xf = din("xf", [128, 64, D])
    ctxb = din("ctxb", [256, D])
    condT = din("condT", [128, 8, 2])
    ada_w = din("ada_w", [2, D, 6 * D])
    ada_b = din("ada_b", [2, 6 * D])
    nmix = din("nmix", [2, D])
    nmlp = din("nmlp", [2, D])
    w1 = din("w1", [2, D, 4 * D])
    w2 = din("w2", [2, 4 * D, D])
    w_in = din("w_in", [D, 2 * D])
    convw = din("convw", [128, 4, 3])
    w_abo = din("w_abo", [D, D])
    w_qkv = din("w_qkv", [D, 3 * D])
    qkg = din("qkg", [128, 2])
    w_nao = din("w_nao", [D, D])
    ident_d = din("ident", [128, 128])
    sel_d = din("sel", [2, 2, 128])
    MT_d = din("MT", [128, 64, 2, 128])
    WC_d = din("WC", [128, 2, 20])
    BCS_d = din("BCS", [128, 2, 128])
    CT_d = din("CT256", [128, 2, 2, 256])
    gm_d = din("gmask", [128, 2])
    EBB_d = din("ebias", [128, 16, 12, 128])
    EBM_d = din("emask", [128, 12, 128])
    RM_d = din("rmask", [128, 4, 6, 128])
    y = dout("y", [2048, D])

    w1b = dscr("w1b", [2, D, 4 * D])
    w2b = dscr("w2b", [2, 4 * D, D])
    Gd = dscr("Gd", [128, 4, 128, 128])
    QTd = dscr("QTd", [8, 128, 2048])
    KTd = dscr("KTd", [8, 128, NALL * 128])
    Vd = dscr("Vd", [8, NALL, 128, 130])
    EBd = dscr("EBd", [8, 128, 2, 12, 128])
    xs_d = dscr("xs_d", [NALL, 128, D], F32)
    wabo_b = dscr("wabo_b", [D, D])
    wnao_b = dscr("wnao_b", [D, D])
    wqkv_b = dscr("wqkv_b", [D, 3 * D])
    ada1_b = dscr("ada1_b", [D, 6 * D])

    dbg_out = {}

    with ExitStack() as es:
        k = K(nc, es)

        sb_cnt = [0]

        def sb(name, shape, dt, stack=None):
            sb_cnt[0] += 1
            return (stack or es).enter_context(nc.sbuf_tensor("%s_%d" % (name, sb_cnt[0]), list(shape), dt))

        banks = [es.enter_context(nc.psum_tensor("ps%d" % i, [128, 512], F32)) for i in range(8)]
        bank_rr = [0]

        def nbank():
            i = bank_rr[0]
            bank_rr[0] = (i + 1) % 8
            return i

        def bkey(i):
            return ("ps", i)

        identb = sb("identb", [128, 128], BF16)
        identf = sb("identf", [128, 128], F32)
        sel2f = sb("sel2f", [2, 2, 128], F32)
        sel2b = sb("sel2b", [2, 2, 128], BF16)
        epsc = sb("epsc", [128, 1], F32)
        junk = sb("junk", [128, D], BF16)
        stat = sb("stat", [128, 8, 2], F32)
        stat_ring = Ring("stat", stat, 8)
        xs_t = sb("xs_t", [128, 5, D], BF16)
        xs_ring = Ring("xs", xs_t, 5)
        condf = sb("condf", [128, 8, 2], F32)
        condb = sb("condb", [128, 8, 2], BF16)
        shT_t = sb("shT_t", [128, 2, 2, 8, 2], BF16)
        rrow_t = sb("rrow", [2, 4, 512], F32)
        bcol_tmp = sb("bcol_tmp", [2, 2, 512], F32)

        k.dma("sp", identf[:], ident_d[:, :], writes=["identf"])
        k.dma("pool", identb[:], ident_d[:, :], writes=["identb"])
        k.dma("sp", sel2f[:], sel_d[:, :, :], writes=["sel2f"])
        k.dma("pool", sel2b[:], sel_d[:, :, :], writes=["sel2b"])
        k.dma("sp", condf[:], condT[:, :, :], writes=["condf"])
        k.op("dve", lambda e: e.memset(epsc[:], EPS), writes=["epsc"])
        k.op("act", lambda e: e.activation(out=condb[:], in_=condf[:], func=AF.Silu),
             reads=["condf"], writes=["condb"])

        def convert_mlp(l):
            for r0 in range(0, D, 256):
                k.dma("pool", w1b[l, r0:r0 + 256, :], w1[l, r0:r0 + 256, :], writes=[("w1b", l, r0 // 256)], bg=True)
            for r0 in range(0, 4 * D, 512):
                k.dma("pool", w2b[l, r0:r0 + 512, :], w2[l, r0:r0 + 512, :], writes=[("w2b", l, r0 // 512)], bg=True)

        def convert_bg(dst, src, key, rows_per):
            R_ = src.shape[0]
            for r0 in range(0, R_, rows_per):
                k.dma("pool", dst[r0:r0 + rows_per, :], src[r0:r0 + rows_per, :], writes=[(key, r0 // rows_per)], bg=True)
            return [(key, i) for i in range(R_ // rows_per)]

        def load_w_bf16(dst, src2d, key, ckeys, rows_per=256):
            R_ = src2d.shape[0]
            for r0 in range(0, R_, rows_per):
                c0, c1 = r0 // 128, (r0 + rows_per) // 128
                k.dma("sp", dst[:, c0:c1, :], src2d[r0:r0 + rows_per, :].rearrange("(c p) n -> p c n", p=128),
                      reads=ckeys, writes=[(key, c) for c in range(c0, c1)])

        def load_w_cast(dst, src2d, key, rows_per=256):
            R = src2d.shape[0]
            for r0 in range(0, R, rows_per):
                c0, c1 = r0 // 128, (r0 + rows_per) // 128
                k.dma("pool", dst[:, c0:c1, :],
                      src2d[r0:r0 + rows_per, :].rearrange("(c p) n -> p c n", p=128),
                      writes=[(key, c) for c in range(c0, c1)])

        evac_rr = [0]

        def evac_eng():
            evac_rr[0] ^= 1
            return "act" if evac_rr[0] else "dve"

        def copy_op(en, out, in_, reads, writes):
            if en == "act":
                return k.op("act", lambda e: e.copy(out=out, in_=in_), reads=reads, writes=writes)
            return k.op(en, lambda e: e.tensor_copy(out=out, in_=in_), reads=reads, writes=writes)

        def norm_tile(x_ap, xkey, S_ap, skey, dst_fn):
            si, skey_stat = stat_ring.next()
            ssq = stat[:, si, 0:1]
            rstd = stat[:, si, 1:2]
            k.op("act", lambda e: e.activation(out=junk[:], in_=x_ap, func=AF.Square, accum_out=ssq),
                 reads=[xkey], writes=["junk", skey_stat])
            k.op("act", lambda e: e.activation(out=rstd, in_=ssq, func=AF.Ln, scale=1.0 / D, bias=epsc[:, 0:1]),
                 reads=[skey_stat, "epsc"], writes=[skey_stat])
            k.op("act", lambda e: e.activation(out=rstd, in_=rstd, func=AF.Exp, scale=-0.5),
                 reads=[skey_stat], writes=[skey_stat])
            xi, xskey = xs_ring.next()
            xs = xs_t[:, xi, :]
            k.op("dve", lambda e: e.scalar_tensor_tensor(out=xs, in0=x_ap, scalar=rstd, in1=S_ap,
                                                         op0=ALU.mult, op1=ALU.mult),
                 reads=[xkey, skey_stat, skey], writes=[xskey])
            b = nbank()
            bv = banks[b][:].bitcast(BF16).rearrange("p (c t) -> p c t", c=8)
            for c in range(8):
                k.op("pe", lambda e, c=c: e.transpose(bv[:, c, :], xs[:, c * 128:(c + 1) * 128], identb[:]),
                     reads=[xskey, "identb"], writes=[bkey(b)], signal=(c == 7))
            dst_fn(bv, bkey(b))

        shT = {}
        BC = {}
        SEG = ["s1", "c1", "g1", "s2", "c2", "g2"]
        for l in range(2):
            for si_, n in enumerate(("s1", "s2")):
                shT[(l, n)] = shT_t[:, l, si_, :, :]

        def modulation(l, segs, variants, bc_alloc, src_bf16=None, src_keys=()):
            with ExitStack() as ms:
                A_t = sb("A_t", [128, 2, 8, 512], BF16, ms)
                A_ring = Ring(("A", l, segs[0]), A_t, 2)
                rr = [0]
                for seg in segs:
                    n = SEG[seg]
                    for half in range(2):
                        ng = seg * 2 + half
                        ai, akey = A_ring.next()
                        if src_bf16 is None:
                            k.dma("pool", A_t[:, ai, :, :],
                                  ada_w[l, :, ng * 512:(ng + 1) * 512].rearrange("(c p) n -> p c n", p=128),
                                  writes=[akey])
                        else:
                            k.dma("sp", A_t[:, ai, :, :],
                                  src_bf16[:, ng * 512:(ng + 1) * 512].rearrange("(c p) n -> p c n", p=128),
                                  reads=list(src_keys), writes=[akey])
                        k.dma("sp", rrow_t[:, 2, :], ada_b[l:l + 1, ng * 512:(ng + 1) * 512].partition_broadcast(2),
                              writes=["adab"])
                        b = nbank()
                        for c in range(8):
                            k.op("pe", lambda e, c=c: e.matmul(banks[b][0:2, :], condb[:, c, :], A_t[:, ai, c, :],
                                                              start=(c == 0), stop=(c == 7)),
                                 reads=["condb", akey], writes=[bkey(b)], signal=(c == 7))
                        ri = rr[0]
                        rr[0] ^= 1
                        rkey = ("rrow", ri)
                        r = rrow_t[:, ri, :]
                        k.op("dve", lambda e: e.tensor_tensor(out=r, in0=banks[b][0:2, :], in1=rrow_t[:, 2, :], op=ALU.add),
                             reads=[bkey(b), "adab"], writes=[rkey])
                        if n in ("c1", "c2"):
                            gsrc = nmix if n == "c1" else nmlp
                            k.dma("sp", rrow_t[:, 3, :],
                                  gsrc[l:l + 1, half * 512:(half + 1) * 512].partition_broadcast(2), writes=["gainp"])
                            k.op("dve", lambda e: e.scalar_tensor_tensor(out=r, in0=r, scalar=1.0, in1=rrow_t[:, 3, :],
                                                                         op0=ALU.add, op1=ALU.mult),
                                 reads=[rkey, "gainp"], writes=[rkey])
                        if n in ("s1", "s2"):
                            b2 = nbank()
                            for j in range(4):
                                k.op("pe", lambda e, j=j: e.transpose(banks[b2][:, j * 2:(j + 1) * 2],
                                                                      r[0:2, j * 128:(j + 1) * 128], identf[0:2, 0:2]),
                                     reads=[rkey, "identf"], writes=[bkey(b2)], signal=(j == 3))
                            dst = shT[(l, n)]
                            k.op("act", lambda e: e.copy(out=dst[:, half * 4:(half + 1) * 4, :],
                                                         in_=banks[b2][:, 0:8].rearrange("p (j v) -> p j v", v=2)),
                                 reads=[bkey(b2)], writes=[("shT", l, n)])
                        else:
                            for v in variants:
                                ap_, key_ = bc_alloc(n, v)
                                b2 = nbank()
                                k.op("pe", lambda e, v=v: e.matmul(banks[b2][:, :], sel2f[:, v, :], r, start=True, stop=True),
                                     reads=["sel2f", rkey], writes=[bkey(b2)])
                                copy_op(evac_eng(), ap_[:, half * 512:(half + 1) * 512], banks[b2][:, :],
                                        [bkey(b2)], [key_])
                k.barrier()

        def bias_rows(shcol, shkey, w_ap_fn, wkeys, ncols, out_fn):
            for g in range((ncols + 511) // 512):
                wdt = min(512, ncols - g * 512)
                b = nbank()
                for c in range(8):
                    k.op("pe", lambda e, c=c: e.matmul(banks[b][0:2, 0:wdt], shcol[:, c, :], w_ap_fn(c, g),
                                                      start=(c == 0), stop=(c == 7)),
                         reads=[shkey] + list(wkeys(g)), writes=[bkey(b)], signal=(c == 7))
                out_fn(g, banks[b][0:2, 0:wdt], bkey(b))

        bcol_rr = [0]

        def rows_to_cols(row_ap, row_key, width, dst_cols, dkey, c0):
            ti = bcol_rr[0]
            bcol_rr[0] ^= 1
            tkey = ("bcol_tmp", ti)
            k.op("dve", lambda e: e.tensor_copy(out=bcol_tmp[:, ti, 0:width], in_=row_ap), reads=[row_key], writes=[tkey])
            b2 = nbank()
            nj = width // 128
            for j in range(nj):
                k.op("pe", lambda e, j=j: e.transpose(banks[b2][:, j * 2:(j + 1) * 2],
                                                      bcol_tmp[0:2, ti, j * 128:(j + 1) * 128], identf[0:2, 0:2]),
                     reads=[tkey, "identf"], writes=[bkey(b2)], signal=(j == nj - 1))
            k.op("act", lambda e: e.copy(out=dst_cols[:, c0:c0 + nj, :],
                                         in_=banks[b2][:, 0:2 * nj].rearrange("p (j v) -> p j v", v=2)),
                 reads=[bkey(b2)], writes=[dkey])

        def dbg_dump(name, src_ap, shape, dt, reads):
            if name in dbg:
                o = dout("dbg_" + name, shape, dt)
                k.dma("sp", o, src_ap, reads=reads, writes=["dbg_" + name])

        L0 = es.enter_context(ExitStack())
        MIX = es.enter_context(ExitStack())
        bc0_t = sb("bc0", [128, 4, D], F32, MIX)
        bc0_names = {}

        def bc0_alloc(n, v):
            kk_ = (n, v)
            if kk_ not in bc0_names:
                bc0_names[kk_] = len(bc0_names)
            i = bc0_names[kk_]
            BC[(0, n, v)] = (bc0_t[:, i, :], ("bc0", i))
            return BC[(0, n, v)]

        modulation(0, [0, 1], [0, 1], bc0_alloc)

        win_t = sb("win_t", [128, 8, 2 * D], BF16, MIX)
        load_w_cast(win_t, w_in, "win")
        WIN_KEYS = [("win", c) for c in range(8)]
        modulation(0, [2], [0, 1], bc0_alloc)
        WABO_KEYS = convert_bg(wabo_b, w_abo, "wabo_b", 512)
        convert_mlp(0)

        bin_cols = sb("bin_cols", [128, 12, 2], F32, MIX)
        bf_row = sb("bf_row", [2, 512], BF16, MIX)
        AY = sb("AY", [128, 8, NTE], BF16, MIX)
        AYc = sb("AYc", [128, 8, 256], BF16, MIX)

        def _bin_out(g, row_ap, row_key):
            if g < 3:
                rows_to_cols(row_ap, row_key, 512, bin_cols, "bin_cols", g * 4)
            else:
                k.op("act", lambda e: e.copy(out=bf_row[:, :], in_=row_ap), reads=[row_key], writes=["bf_row"])

        bias_rows(shT[(0, "s1")], ("shT", 0, "s1"), lambda c, g: win_t[:, c, g * 512:(g + 1) * 512],
                  lambda g: WIN_KEYS, 2 * D, _bin_out)

        WC_t = sb("WC_t", [128, 2, 20], BF16, MIX)
        BCS_t = sb("BCS_t", [128, 2, 128], BF16, MIX)
        CT_t = sb("CT_t", [128, 2, 2, 256], BF16, MIX)
        cw_t = sb("cw_t", [128, 4, 3], F32, MIX)
        gm_t = sb("gm_t", [128, 2], F32, MIX)
        k.dma("pool", WC_t[:], WC_d[:, :, :], writes=["WC"])
        k.dma("pool", BCS_t[:], BCS_d[:, :, :], writes=["BCS"])
        k.dma("pool", CT_t[:], CT_d[:, :, :, :], writes=["CT"])
        k.dma("sp", cw_t[:], convw[:, :, :], writes=["cw"])
        k.dma("sp", gm_t[:], gm_d[:, :], writes=["gm"])

        CD = es.enter_context(ExitStack())
        hTe = sb("hTe", [128, 8, NTE + 2], BF16, CD)
        PC = es.enter_context(ExitStack())
        MT_t = sb("MT_t", [128, 6, 2, 128], BF16, PC)
        MT_ring = Ring("MT", MT_t, 6)
        xc_t = sb("xc_t", [128, 3, D], F32, PC)
        xc_ring = Ring("xc", xc_t, 3)
        hTt = sb("hTt", [128, 3, 8, 128], BF16, PC)
        hTt_ring = Ring("hTt", hTt, 3)
        f_t = sb("f_t", [128, 3, 512], BF16, PC)
        f_ring = Ring("f", f_t, 3)
        G_t = sb("G_t", [128, 3, 2, 512], BF16, PC)
        G_ring = Ring("G", G_t, 3)
        S1x, S1xk = BC[(0, "c1", 0)]
        k.op("pool", lambda e: e.memset(hTe[:, :, NTE + 1:NTE + 2], 0.0), writes=["hTe_tail"])

        def c_s0(t2, st):
            xi, xkey = xc_ring.next()
            k.dma("sp", xc_t[:, xi, :], xf[:, t2, :], writes=[xkey])
            mi, mkey = MT_ring.next()
            k.dma("pool", MT_t[:, mi, :, :], MT_d[:, t2, :, :], writes=[mkey])
            st["mi"], st["mkey"] = mi, mkey
            x_ap = xc_t[:, xi, :]
            si, skey_stat = stat_ring.next()
            ssq = stat[:, si, 0:1]
            rstd = stat[:, si, 1:2]
            k.op("act", lambda e: e.activation(out=junk[:], in_=x_ap, func=AF.Square, accum_out=ssq),
                 reads=[xkey], writes=["junk", skey_stat])
            k.op("act", lambda e: e.activation(out=rstd, in_=ssq, func=AF.Ln, scale=1.0 / D, bias=epsc[:, 0:1]),
                 reads=[skey_stat, "epsc"], writes=[skey_stat])
            k.op("act", lambda e: e.activation(out=rstd, in_=rstd, func=AF.Exp, scale=-0.5),
                 reads=[skey_stat], writes=[skey_stat])
            qi, qkey = xs_ring.next()
            k.op("dve", lambda e: e.scalar_tensor_tensor(out=xs_t[:, qi, :], in0=x_ap, scalar=rstd, in1=S1x,
                                                         op0=ALU.mult, op1=ALU.mult),
                 reads=[xkey, skey_stat, S1xk], writes=[qkey])
            st["qi"], st["qkey"] = qi, qkey

        def c_s1(t2, st):
            qi, qkey = st["qi"], st["qkey"]
            b = nbank()
            bv = banks[b][:].bitcast(BF16).rearrange("p (c t) -> p c t", c=8)
            for c in range(8):
                k.op("pe", lambda e, c=c: e.transpose(bv[:, c, :], xs_t[:, qi, c * 128:(c + 1) * 128], identb[:]),
                     reads=[qkey, "identb"], writes=[bkey(b)], signal=(c == 7))
            hi, hkey = hTt_ring.next()
            st["hi"], st["hkey"] = hi, hkey
            copy_op("act", hTt[:, hi, :, :], bv, [bkey(b)], [hkey])
            k.op("pool", lambda e: e.tensor_copy(out=hTe[:, :, 1 + t2:1 + t2 + 64 * 39 + 1:64],
                                                 in_=hTt[:, hi, :, 0:40]), reads=[hkey], writes=[("hTe", t2)])
            if t2 == 63:
                k.op("pool", lambda e: e.tensor_copy(out=hTe[:, :, 0:1], in_=hTt[:, hi, :, 127:128]),
                     reads=[hkey], writes=["hTe_halo"])

        def c_s2(t2, st):
            hi, hkey = st["hi"], st["hkey"]
            b = nbank()
            for c in range(8):
                k.op("pe", lambda e, c=c: e.matmul(banks[b][:, :], hTt[:, hi, c, :], win_t[:, c, 1536:2048],
                                                  start=(c == 0), stop=False),
                     reads=[hkey, ("win", c)], writes=[bkey(b)], signal=False)
            k.op("pe", lambda e: e.matmul(banks[b][:, :], sel2b[:, 0, :], bf_row[:, :], start=False, stop=True),
                 reads=["sel2b", "bf_row"], writes=[bkey(b)])
            fi, fkey = f_ring.next()
            st["fi"], st["fkey"] = fi, fkey
            copy_op("dve", f_t[:, fi, :], banks[b][:, :], [bkey(b)], [fkey])

        def c_s3(t2, st):
            fi, fkey, mi, mkey = st["fi"], st["fkey"], st["mi"], st["mkey"]
            gi, gkey = G_ring.next()
            for ri in range(2):
                b2 = nbank()
                k.op("pe", lambda e, ri=ri, b2=b2: e.matmul(banks[b2][:, :], MT_t[:, mi, ri, :], f_t[:, fi, :],
                                                            start=True, stop=True),
                     reads=[fkey, mkey], writes=[bkey(b2)])
                copy_op("act" if ri == 0 else "dve", G_t[:, gi, ri, :], banks[b2][:, :], [bkey(b2)], [(gkey, ri)])
            for ri in range(2):
                k.dma("act", Gd[ri * 64 + t2, :, :, :].rearrange("c k h -> k c h"),
                      G_t[:, gi, ri, :].rearrange("p (c h) -> p c h", c=4),
                      reads=[(gkey, ri)], writes=[("Gd", ri, t2)])

        run_pipeline(64, [c_s0, c_s1, c_s2, c_s3], [0, 1, 2, 3])
        k.barrier()
        PC.close()
        ADA1_KEYS = []
        for r0 in range(0, D, 256):
            k.dma("pool", ada1_b[r0:r0 + 256, 0:2 * D], ada_w[1, r0:r0 + 256, 0:2 * D], writes=[("ada1_b", r0 // 256)], bg=True)
            ADA1_KEYS.append(("ada1_b", r0 // 256))
        WQKV_KEYS = convert_bg(wqkv_b, w_qkv, "wqkv_b", 256)
        WNAO_KEYS = convert_bg(wnao_b, w_nao, "wnao_b", 512)
        HTE_KEYS = [("hTe", t) for t in range(64)] + ["hTe_halo", "hTe_tail"]
        GD_KEYS = [("Gd", ri, t) for ri in range(2) for t in range(64)]
        dbg_dump("hTe", hTe[:, :, :], [128, 8, NTE + 2], BF16, HTE_KEYS)
        dbg_dump("Gd", Gd[:, :, :, :], [128, 4, 128, 128], BF16, GD_KEYS)
        dbg_out["bc0_names"] = dict(bc0_names)
        if stop == "C":
            k.finish()
            print("ops", k.n_ops, "waits", k.n_waits)
            return nc, dbg_out
        def conv_mixer_a(hT_ap, hkeys, ntok, nz, z_bufs, ax_ring, ax_t, tmp_t, tmp_ring, v, dst, dkey_fn, ghost):
            ngrp = (ntok + 511) // 512
            nzg = (nz + 511) // 512
            def zloop(cc):
                z_t = z_bufs[cc % 2]
                zkey = ("zb", cc % 2)
                k.op("pool", lambda e: e.memset(z_t[:, nz:ntok + 2], 0.0), writes=[("ztail", cc % 2)])
                for g in range(nzg):
                    c0 = g * 512
                    wd = min(512, nz - c0)
                    bx = nbank()
                    for c in range(8):
                        k.op("pe", lambda e, c=c: e.matmul(banks[bx][:, 0:wd], win_t[:, c, cc * 128:(cc + 1) * 128],
                                                          hT_ap[:, c, c0:c0 + wd], start=(c == 0), stop=(c == 7)),
                             reads=[("win", c)] + hkeys, writes=[bkey(bx)], signal=(c == 7))
                    ai, akey = ax_ring.next()
                    k.op("act", lambda e: e.activation(out=ax_t[:, ai, 0:wd], in_=banks[bx][:, 0:wd], func=AF.Identity,
                                                       bias=bin_cols[:, cc, v:v + 1]),
                         reads=[bkey(bx), "bin_cols"], writes=[akey])
                    bc_ = nbank()
                    for c in range(8):
                        k.op("pe", lambda e, c=c: e.matmul(banks[bc_][:, 0:wd],
                                                          win_t[:, c, 512 + cc * 128:512 + (cc + 1) * 128],
                                                          hT_ap[:, c, c0:c0 + wd], start=(c == 0), stop=(c == 7)),
                             reads=[("win", c)] + hkeys, writes=[bkey(bc_)], signal=(c == 7))
                    k.op("dve", lambda e: e.scalar_tensor_tensor(out=z_t[:, c0:c0 + wd], in0=banks[bc_][:, 0:wd],
                                                                 scalar=bin_cols[:, 4 + cc, v:v + 1], in1=ax_t[:, ai, 0:wd],
                                                                 op0=ALU.add, op1=ALU.mult),
                         reads=[bkey(bc_), akey, "bin_cols"], writes=[(zkey, g)])
                zk_all = [(zkey, g) for g in range(nzg)] + [("ztail", cc % 2)]
                if ghost == "ext":
                    for gi_, col in ((0, 256), (1, 2305)):
                        k.op("dve", lambda e, gi_=gi_, col=col: e.tensor_scalar(
                            z_t[:, col:col + 1], z_t[:, col:col + 1], gm_t[:, gi_:gi_ + 1], None, op0=ALU.mult),
                            reads=zk_all + ["gm"], writes=[(zkey, col // 512)])
                else:
                    k.op("pool", lambda e: e.memset(z_t[:, 0:1], 0.0), reads=zk_all, writes=[(zkey, 0)])

            def convloop(cc):
                z_t = z_bufs[cc % 2]
                zkey = ("zb", cc % 2)
                zk_all = [(zkey, g) for g in range(nzg)] + [("ztail", cc % 2)]
                for g in range(ngrp):
                    e0 = g * 512
                    wd = min(512, ntok - e0)
                    ti, tkey = tmp_ring.next()
                    tt_ = tmp_t[:, ti, 0:wd]
                    k.op("dve", lambda e: e.tensor_scalar(tt_, z_t[:, 1 + e0:1 + e0 + wd], cw_t[:, cc, 1:2], None,
                                                           op0=ALU.mult), reads=zk_all + ["cw"], writes=[tkey])
                    k.op("dve", lambda e: e.scalar_tensor_tensor(out=tt_, in0=z_t[:, e0:e0 + wd], scalar=cw_t[:, cc, 0:1],
                                                                  in1=tt_, op0=ALU.mult, op1=ALU.add),
                         reads=zk_all + [tkey], writes=[tkey])
                    k.op("dve", lambda e: e.scalar_tensor_tensor(out=tt_, in0=z_t[:, 2 + e0:2 + e0 + wd],
                                                                  scalar=cw_t[:, cc, 2:3], in1=tt_, op0=ALU.mult, op1=ALU.add),
                         reads=zk_all + [tkey], writes=[tkey])
                    bb = nbank()
                    for c in range(8):
                        k.op("pe", lambda e, c=c: e.matmul(banks[bb][:, 0:wd],
                                                          win_t[:, c, 1024 + cc * 128:1024 + (cc + 1) * 128],
                                                          hT_ap[:, c, 1 + e0:1 + e0 + wd], start=(c == 0), stop=(c == 7)),
                             reads=[("win", c)] + hkeys, writes=[bkey(bb)], signal=(c == 7))
                    k.op("dve", lambda e: e.scalar_tensor_tensor(out=dst[:, cc, e0:e0 + wd], in0=banks[bb][:, 0:wd],
                                                                 scalar=bin_cols[:, 8 + cc, v:v + 1], in1=tt_,
                                                                 op0=ALU.add, op1=ALU.mult),
                         reads=[bkey(bb), tkey, "bin_cols"], writes=[dkey_fn(cc, g)])


            zloop(0)
            for cc in range(4):
                if cc + 1 < 4:
                    zloop(cc + 1)
                convloop(cc)

        PD = es.enter_context(ExitStack())
        z_t = sb("z_t", [128, 2, NTE + 2], BF16, PD)
        ax_t = sb("ax_t", [128, 3, 512], BF16, PD)
        ax_ring = Ring("ax", ax_t, 3)
        ctmp_t = sb("ctmp_t", [128, 3, 512], F32, PD)
        ctmp_ring = Ring("ctmp", ctmp_t, 3)
        conv_mixer_a(hTe, HTE_KEYS, NTE, NTE, [z_t[:, 0, :], z_t[:, 1, :]], ax_ring, ax_t, ctmp_t, ctmp_ring, 0, AY, lambda cc, g: ("AY", cc, g), "ext")
        k.barrier()
        PD.close()
        CD.close()
        dbg_dump("bin_cols", bin_cols[:, :, :], [128, 12, 2], F32, ["bin_cols"])
        AY_KEYS_A = [("AY", cc, g) for cc in range(4) for g in range(5)]
        dbg_dump("AYa", AY[:, 0:4, :], [128, 4, NTE], BF16, AY_KEYS_A)
        if stop == "D":
            k.finish()
            return nc, dbg_out

        PE_ = es.enter_context(ExitStack())
        R_t = sb("R_t", [128, 2, 64, 128], BF16, PE_)
        R_ring = Ring("R", R_t, 2)
        Z_t = sb("Z_t", [128, 2, 2, NTE], BF16, PE_)
        Z_ring = Ring("Z", Z_t, 2)
        for cc in range(4):
            zi, zkey = Z_ring.next()
            Zv = Z_t[:, zi, :, :].rearrange("p r (i q) -> p q r i", q=128)
            for half in range(2):
                ri_, rkey = R_ring.next()
                k.dma("sp", R_t[:, ri_, :, :], Gd[:, cc, half * 64:(half + 1) * 64, :], reads=GD_KEYS, writes=[rkey])
                for g0 in range(0, 64, 12):
                    nj = min(12, 64 - g0)
                    b = nbank()
                    for j in range(nj):
                        k.op("pe", lambda e, j=j: e.matmul(banks[b][:, j * 40:(j + 1) * 40], R_t[:, ri_, g0 + j, :],
                                                          WC_t[:].rearrange("p r i -> p (r i)"), start=True, stop=True),
                             reads=[rkey, "WC"], writes=[bkey(b)], signal=(j == nj - 1))
                    k1_0 = half * 64 + g0
                    copy_op(evac_eng(), Zv[:, k1_0:k1_0 + nj, :, :],
                            banks[b][:, 0:nj * 40].rearrange("p (j r i) -> p j r i", r=2, i=20),
                            [bkey(b)], [(zkey, half, g0)])
            zk_all = [(zkey, h_, g_) for h_ in range(2) for g_ in range(0, 64, 12)]
            for g in range(5):
                b = nbank()
                for cs in range(2):
                    k.op("pe", lambda e, cs=cs: e.matmul(banks[b][:, :], BCS_t[:, cs, :], Z_t[:, zi, cs, g * 512:(g + 1) * 512],
                                                        start=(cs == 0), stop=(cs == 1)),
                         reads=zk_all + ["BCS"], writes=[bkey(b)], signal=(cs == 1))
                copy_op(evac_eng(), AY[:, 4 + cc, g * 512:(g + 1) * 512], banks[b][:, :], [bkey(b)], [("AY", 4 + cc, g)])
        k.barrier()
        PE_.close()
        AY_KEYS = [("AY", cc, g) for cc in range(8) for g in range(5)]
        dbg_dump("AY", AY[:, :, :], [128, 8, NTE], BF16, AY_KEYS)
        if stop == "E":
            k.finish()
            return nc, dbg_out

        PG = es.enter_context(ExitStack())
        hTc = sb("hTc", [128, 8, 258], BF16, PG)
        zc_t = sb("zc_t", [128, 2, 258], BF16, PG)
        axc_t = sb("axc_t", [128, 2, 512], BF16, PG)
        axc_ring = Ring("axc", axc_t, 2)
        ctmpc_t = sb("ctmpc_t", [128, 2, 512], F32, PG)
        ctmpc_ring = Ring("ctmpc", ctmpc_t, 2)
        xg_t = sb("xg0_t", [128, 4, D], F32, PG)
        xg_ring = Ring("xg0", xg_t, 4)
        fc_t = sb("fc_t", [128, 2, 512], BF16, PG)
        Zc_t = sb("Zc_t", [128, 2, 2, 256], BF16, PG)
        Zc_ring = Ring("Zc", Zc_t, 2)
        Wo_t = sb("Wo_t", [128, 8, D], BF16, PG)
        load_w_bf16(Wo_t, wabo_b, "Wo", WABO_KEYS)
        WO_KEYS = [("Wo", c) for c in range(8)]
        S1c, S1ck = BC[(0, "c1", 1)]
        k.op("pool", lambda e: e.memset(hTc[:, :, 0:1], 0.0), writes=["hTc_pad0"])
        k.op("pool", lambda e: e.memset(hTc[:, :, 257:258], 0.0), writes=["hTc_pad1"])
        for j in range(NCTX):
            xi, xkey = xg_ring.next()
            k.dma("sp", xg_t[:, xi, :], ctxb[j * 128:(j + 1) * 128, :], writes=[xkey])

            def _evc(bv, bk, j=j):
                copy_op("act", hTc[:, :, 1 + j * 128:1 + (j + 1) * 128], bv, [bk], [("hTc", j)])
            norm_tile(xg_t[:, xi, :], xkey, S1c, S1ck, _evc)
        HTC_KEYS = [("hTc", 0), ("hTc", 1), "hTc_pad0", "hTc_pad1"]
        conv_mixer_a(hTc, HTC_KEYS, 256, 257, [zc_t[:, 0, :], zc_t[:, 1, :]], axc_ring, axc_t, ctmpc_t, ctmpc_ring, 1, AYc,
                     lambda cc, g: ("AYc", cc), "ctx")
        for j in range(NCTX):
            b = nbank()
            for c in range(8):
                k.op("pe", lambda e, c=c: e.matmul(banks[b][:, :], hTc[:, c, 1 + j * 128:1 + (j + 1) * 128],
                                                  win_t[:, c, 1536:2048], start=(c == 0), stop=False),
                     reads=[("hTc", j), ("win", c)], writes=[bkey(b)], signal=False)
            k.op("pe", lambda e: e.matmul(banks[b][:, :], sel2b[:, 1, :], bf_row[:, :], start=False, stop=True),
                 reads=["sel2b", "bf_row"], writes=[bkey(b)])
            copy_op(evac_eng(), fc_t[:, j, :], banks[b][:, :], [bkey(b)], [("fc", j)])
        for cc in range(4):
            zi, zkey = Zc_ring.next()
            b = nbank()
            for cs in range(2):
                for tt in range(2):
                    k.op("pe", lambda e, cs=cs, tt=tt: e.matmul(banks[b][:, cs * 256:(cs + 1) * 256],
                                                               fc_t[:, tt, cc * 128:(cc + 1) * 128], CT_t[:, tt, cs, :],
                                                               start=(tt == 0), stop=(tt == 1)),
                         reads=[("fc", tt), "CT"], writes=[bkey(b)], signal=(cs == 1 and tt == 1))
            copy_op(evac_eng(), Zc_t[:, zi, :, :], banks[b][:, :].rearrange("p (r q) -> p r q", r=2), [bkey(b)], [zkey])
            b2 = nbank()
            for cs in range(2):
                k.op("pe", lambda e, cs=cs: e.matmul(banks[b2][:, 0:256], BCS_t[:, cs, :], Zc_t[:, zi, cs, :],
                                                    start=(cs == 0), stop=(cs == 1)),
                     reads=[zkey, "BCS"], writes=[bkey(b2)], signal=(cs == 1))
            copy_op(evac_eng(), AYc[:, 4 + cc, :], banks[b2][:, 0:256], [bkey(b2)], [("AYc", 4 + cc)])
        AYC_KEYS = [("AYc", c) for c in range(8)]
        dbg_dump("AYc", AYc[:, :, :], [128, 8, 256], BF16, AYC_KEYS)

        gtmp_t = sb("gtmp_t", [128, 2, 512], F32, PG)
        gtmp_ring = Ring("gtmp", gtmp_t, 2)

        def out_proj_residual(src, skeys_fn, tok0, xslot, G_ap, gkey, dst_dram, dkey):
            xi, xkey = xslot
            for nh in range(2):
                b = nbank()
                for kc in range(8):
                    k.op("pe", lambda e, kc=kc: e.matmul(banks[b][:, :], src[:, kc, tok0:tok0 + 128],
                                                        Wo_t[:, kc, nh * 512:(nh + 1) * 512], start=(kc == 0), stop=(kc == 7)),
                         reads=skeys_fn(kc) + [("Wo", kc)], writes=[bkey(b)], signal=(kc == 7))
                ti, tkey = gtmp_ring.next()
                k.op("dve", lambda e: e.tensor_tensor(out=gtmp_t[:, ti, :], in0=banks[b][:, :],
                                                      in1=G_ap[:, nh * 512:(nh + 1) * 512], op=ALU.mult),
                     reads=[bkey(b), gkey], writes=[tkey])
                k.op("pool", lambda e: e.tensor_tensor(out=xg_t[:, xi, nh * 512:(nh + 1) * 512],
                                                       in0=xg_t[:, xi, nh * 512:(nh + 1) * 512], in1=gtmp_t[:, ti, :], op=ALU.add),
                     reads=[tkey, xkey], writes=[xkey])
            k.dma("act", dst_dram, xg_t[:, xi, :], reads=[xkey], writes=[dkey])

        G1x, G1xk = BC[(0, "g1", 0)]
        G1c, G1ck = BC[(0, "g1", 1)]

        def f_src(t):
            if t < NEXT:
                return xf[2 * t:2 * t + 2, :, :].rearrange("p t d -> (p t) d")
            return ctxb[(t - NEXT) * 128:(t - NEXT + 1) * 128, :]

        fslots = {}

        def f_load(t):
            xi, xkey = xg_ring.next()
            k.dma("sp", xg_t[:, xi, :], f_src(t), writes=[xkey])
            fslots[t] = (xi, xkey)

        f_load(0)
        f_load(1)
        for t in range(NALL):
            if t + 2 < NALL:
                f_load(t + 2)
            if t < NEXT:
                out_proj_residual(AY, lambda kc, t=t: [("AY", kc, t // 4)], t * 128, fslots[t], G1x, G1xk,
                                  xs_d[t, :, :], ("xs", t))
            else:
                out_proj_residual(AYc, lambda kc: [("AYc", kc)], (t - NEXT) * 128, fslots[t], G1c, G1ck,
                                  xs_d[t, :, :], ("xs", t))
        k.barrier()
        PG.close()
        MIX.close()
        XS_KEYS = [("xs", i) for i in range(NALL)]
        dbg_dump("x1m", xs_d[:, :, :], [NALL, 128, D], F32, XS_KEYS)
        if stop == "F":
            k.finish()
            return nc, dbg_out
        def mlp_layer(l, groups, S_of, G_of, dst_fn, dkey_fn, lstack):
            xm_t = sb("xm_t", [128, 8, D], F32, lstack)
            xm_ring = Ring(("xm", l), xm_t, 8)
            xsm_t = sb("xsm_t", [128, 5, D], BF16, lstack)
            xsm_ring = Ring(("xsm", l), xsm_t, 5)
            hTg = sb("hTg", [128, 2, 8, 512], BF16, lstack)
            h1T = sb("h1T", [128, 32, 512], BF16, lstack)
            W1p = sb("W1p", [128, 3, 8, 512], BF16, lstack)
            W1_ring = Ring(("W1p", l), W1p, 3)
            W2p = sb("W2p", [128, 3, 4, D], BF16, lstack)
            W2_ring = Ring(("W2p", l), W2p, 3)
            rt_t = sb("rt_t", [128, 3, 512], BF16, lstack)
            rt_ring = Ring(("rt", l), rt_t, 3)
            mt_t = sb("mt_t", [128, 3, 512], F32, lstack)
            mt_ring = Ring(("mt", l), mt_t, 3)
            b1c = sb("b1c", [128, 32, 2], F32, lstack)
            shc, shk = shT[(l, "s2")], ("shT", l, "s2")

            plan = []
            for gi in range(len(groups)):
                plan += [("w1", gi, pw) for pw in range(8)] + [("w2", gi, pw) for pw in range(8)]
            issued = [0]
            slot = {}

            def ensure(n):
                while issued[0] < min(n + 1, len(plan)):
                    kind, gi, pw = plan[issued[0]]
                    if kind == "w1":
                        si, skey = W1_ring.next()
                        k.dma("sp", W1p[:, si, :, :],
                              w1b[l, :, pw * 512:(pw + 1) * 512].rearrange("(c p) n -> p c n", p=128),
                              reads=[("w1b", l, q) for q in range(4)], writes=[skey])
                    else:
                        si, skey = W2_ring.next()
                        k.dma("sp", W2p[:, si, :, :],
                              w2b[l, pw * 512:(pw + 1) * 512, :].rearrange("(c p) n -> p c n", p=128),
                              reads=[("w2b", l, pw)], writes=[skey])
                    slot[issued[0]] = (si, skey)
                    issued[0] += 1

            xtiles = {}

            def load_group(gi):
                tiles, v = groups[gi]
                for t in tiles:
                    xi, xkey = xm_ring.next()
                    k.dma("sp", xm_t[:, xi, :], xs_d[t, :, :], reads=[("xs", t)], writes=[xkey])
                    xtiles[(gi, t)] = (xi, xkey)

            def norm_pre(gi):
                tiles, v = groups[gi]
                S_ap, skey = S_of(v)
                out = []
                for t in tiles:
                    xi, xkey = xtiles[(gi, t)]
                    x_ap = xm_t[:, xi, :]
                    si, skey_stat = stat_ring.next()
                    ssq = stat[:, si, 0:1]
                    rstd = stat[:, si, 1:2]
                    k.op("act", lambda e: e.activation(out=junk[:], in_=x_ap, func=AF.Square, accum_out=ssq),
                         reads=[xkey], writes=["junk", skey_stat])
                    k.op("act", lambda e: e.activation(out=rstd, in_=ssq, func=AF.Ln, scale=1.0 / D, bias=epsc[:, 0:1]),
                         reads=[skey_stat, "epsc"], writes=[skey_stat])
                    k.op("act", lambda e: e.activation(out=rstd, in_=rstd, func=AF.Exp, scale=-0.5),
                         reads=[skey_stat], writes=[skey_stat])
                    qi, qkey = xsm_ring.next()
                    k.op("dve", lambda e: e.scalar_tensor_tensor(out=xsm_t[:, qi, :], in0=x_ap, scalar=rstd, in1=S_ap,
                                                                 op0=ALU.mult, op1=ALU.mult),
                         reads=[xkey, skey_stat, skey], writes=[qkey])
                    out.append((qi, qkey))
                return out

            def norm_T(gi, pre):
                hb = gi % 2
                for ti, (qi, qkey) in enumerate(pre):
                    b = nbank()
                    bv = banks[b][:].bitcast(BF16).rearrange("p (c t) -> p c t", c=8)
                    for c in range(8):
                        k.op("pe", lambda e, c=c: e.transpose(bv[:, c, :], xsm_t[:, qi, c * 128:(c + 1) * 128], identb[:]),
                             reads=[qkey, "identb"], writes=[bkey(b)], signal=(c == 7))
                    copy_op(evac_eng(), hTg[:, hb, :, ti * 128:(ti + 1) * 128], bv, [bkey(b)], [("hTg", l, hb, ti)])

            load_group(0)
            ensure(1)
            pre = norm_pre(0)
            norm_T(0, pre)
            pidx = 0
            for gi, (tiles, v) in enumerate(groups):
                nt = len(tiles)
                ntok = nt * 128
                hb = gi % 2
                hkeys = [("hTg", l, hb, ti) for ti in range(nt)]
                if gi + 1 < len(groups):
                    load_group(gi + 1)
                for pw in range(8):
                    ensure(pidx + 2)
                    si, skey = slot[pidx]
                    pidx += 1
                    if gi == 0:
                        bias_rows(shc, shk, lambda c, g: W1p[:, si, c, :], lambda g: [skey], 512,
                                  lambda g, row_ap, row_key: rows_to_cols(row_ap, row_key, 512, b1c, ("b1c", l, pw), pw * 4))
                    for ffc in range(4):
                        fg = pw * 4 + ffc
                        b = nbank()
                        for c in range(8):
                            k.op("pe", lambda e, c=c: e.matmul(banks[b][:, 0:ntok], W1p[:, si, c, ffc * 128:(ffc + 1) * 128],
                                                              hTg[:, hb, c, 0:ntok], start=(c == 0), stop=(c == 7)),
                                 reads=[skey] + hkeys, writes=[bkey(b)], signal=(c == 7))
                        ri, rkey = rt_ring.next()
                        k.op("act", lambda e: e.activation(out=rt_t[:, ri, 0:ntok], in_=banks[b][:, 0:ntok], func=AF.Relu,
                                                           bias=b1c[:, fg, v:v + 1]),
                             reads=[bkey(b), ("b1c", l, pw)], writes=[rkey])
                        k.op("pool", lambda e: e.tensor_tensor(out=h1T[:, fg, 0:ntok], in0=rt_t[:, ri, 0:ntok],
                                                               in1=rt_t[:, ri, 0:ntok], op=ALU.mult),
                             reads=[rkey], writes=[("h1T", l, fg)])
                if gi + 1 < len(groups):
                    pre = norm_pre(gi + 1)
                for pw in range(8):
                    ensure(pidx + 2)
                    si, skey = slot[pidx]
                    pidx += 1
                    for ti in range(nt):
                        for nh in range(2):
                            b = ti * 2 + nh
                            for ffc in range(4):
                                fg = pw * 4 + ffc
                                first = (pw == 0 and ffc == 0)
                                last = (pw == 7 and ffc == 3)
                                k.op("pe", lambda e, ffc=ffc, fg=fg, first=first, last=last:
                                     e.matmul(banks[b][:, :], h1T[:, fg, ti * 128:(ti + 1) * 128],
                                              W2p[:, si, ffc, nh * 512:(nh + 1) * 512], start=first, stop=last),
                                     reads=[skey, ("h1T", l, fg)], writes=[bkey(b)],
                                     signal=(last or (ti == nt - 1 and nh == 1 and ffc == 3)))
                G_ap, gkey = G_of(v)
                for ti, t in enumerate(tiles):
                    xi, xkey = xtiles[(gi, t)]
                    for nh in range(2):
                        b = ti * 2 + nh
                        mi, mkey = mt_ring.next()
                        k.op("dve", lambda e: e.tensor_tensor(out=mt_t[:, mi, :], in0=banks[b][:, :],
                                                              in1=G_ap[:, nh * 512:(nh + 1) * 512], op=ALU.mult),
                             reads=[bkey(b), gkey], writes=[mkey])
                        k.op("pool", lambda e: e.tensor_tensor(out=xm_t[:, xi, nh * 512:(nh + 1) * 512],
                                                               in0=xm_t[:, xi, nh * 512:(nh + 1) * 512],
                                                               in1=mt_t[:, mi, :], op=ALU.add),
                             reads=[mkey, xkey], writes=[xkey])
                    k.dma("sp", dst_fn(t), xm_t[:, xi, :], reads=[xkey], writes=[dkey_fn(t)])
                if gi + 1 < len(groups):
                    norm_T(gi + 1, pre)

        ML0 = es.enter_context(ExitStack())
        bc0b_t = sb("bc0b", [128, 4, D], F32, ML0)
        bc0b_names = {}

        def bc0b_alloc(n, v):
            kk_ = (n, v)
            if kk_ not in bc0b_names:
                bc0b_names[kk_] = len(bc0b_names)
            i = bc0b_names[kk_]
            BC[(0, n, v)] = (bc0b_t[:, i, :], ("bc0b", i))
            return BC[(0, n, v)]

        modulation(0, [3, 4, 5], [0, 1], bc0b_alloc)
        groups0 = [([4 * g + t for t in range(4)], 0) for g in range(5)] + [([20, 21], 1)]
        mlp_layer(0, groups0, lambda v: BC[(0, "c2", v)], lambda v: BC[(0, "g2", v)],
                  lambda t: xs_d[t, :, :], lambda t: ("xs", t), ML0)
        k.barrier()
        ML0.close()
        L0.close()
        dbg_dump("x1", xs_d[:, :, :], [NALL, 128, D], F32, XS_KEYS)
        if stop == "L0":
            k.finish()
            return nc, dbg_out
        L1 = es.enter_context(ExitStack())
        bc1_t = sb("bc1", [128, 5, D], F32, L1)
        bc1_names = {}

        def bc1_alloc(n, v):
            kk_ = (n, v)
            if kk_ not in bc1_names:
                bc1_names[kk_] = len(bc1_names)
            i = bc1_names[kk_]
            BC[(1, n, v)] = (bc1_t[:, i, :], ("bc1", i))
            return BC[(1, n, v)]

        modulation(1, [0, 1], [0, 1], bc1_alloc, src_bf16=ada1_b, src_keys=ADA1_KEYS)

        AT = es.enter_context(ExitStack())
        em_t = sb("em_t", [128, 12, 128], BF16, AT)
        k.dma("pool", em_t[:], EBM_d[:, :, :], writes=["em"])
        ebf_t = sb("ebf_t", [128, 2, 12, 128], F32, AT)
        ebf_ring = Ring("ebf", ebf_t, 2)
        ebb_t = sb("ebb_t", [128, 2, 12, 128], BF16, AT)
        ebb_ring = Ring("ebb", ebb_t, 2)
        def eb_phase():
            slots = {}

            def eb_load(h):
                fi, fkey = ebf_ring.next()
                k.dma("act", ebf_t[:, fi, :, :], EBB_d[:, h, :, :], writes=[fkey])
                slots[h] = (fi, fkey)

            eb_load(0)
            for h in range(16):
                if h + 1 < 16:
                    eb_load(h + 1)
                fi, fkey = slots[h]
                bi, bkey_ = ebb_ring.next()
                k.op("act", lambda e: e.activation(out=ebb_t[:, bi, :, :], in_=ebf_t[:, fi, :, :], func=AF.Exp),
                     reads=[fkey], writes=[bkey_])
                k.op("pool", lambda e: e.tensor_tensor(out=ebb_t[:, bi, :, :], in0=ebb_t[:, bi, :, :], in1=em_t[:, :, :],
                                                       op=ALU.mult), reads=[bkey_, "em"], writes=[bkey_])
                k.dma("pool", EBd[h // 2, :, h % 2, :, :], ebb_t[:, bi, :, :], reads=[bkey_], writes=[("EBd", h)])

        QK = es.enter_context(ExitStack())
        wq_t = sb("wq_t", [128, 8, 3 * D], BF16, QK)
        for cb in (1, 0, 2):
            for r0 in range(0, D, 512):
                c0, c1 = r0 // 128, (r0 + 512) // 128
                k.dma("sp", wq_t[:, c0:c1, cb * D:(cb + 1) * D],
                      wqkv_b[r0:r0 + 512, cb * D:(cb + 1) * D].rearrange("(c p) n -> p c n", p=128),
                      reads=WQKV_KEYS, writes=[("wq", cb, c) for c in range(c0, c1)])
        WQ_KEYS = [("wq", cb, c) for cb in range(3) for c in range(8)]
        convert_mlp(1)
        bq_cols = sb("bq_cols", [128, 16, 2], F32, QK)
        bv_row = sb("bv_row", [2, D], BF16, QK)
        qkg_t = sb("qkg_t", [128, 2], F32, QK)
        blk1 = sb("blk1", [128, 128], BF16, QK)
        eps64 = sb("eps64", [128, 1], F32, QK)
        k.dma("sp", qkg_t[:], qkg[:, :], writes=["qkg"])
        k.op("dve", lambda e: e.tensor_scalar(qkg_t[:, 0:1], qkg_t[:, 0:1], 0.125, None, op0=ALU.mult),
             reads=["qkg"], writes=["qkg"])
        k.op("pool", lambda e: e.memset(blk1[:], 0.0), writes=["blk1"])
        k.op("pool", lambda e: e.memset(blk1[0:64, 0:64], 1.0), writes=["blk1"])
        k.op("pool", lambda e: e.memset(blk1[64:128, 64:128], 1.0), writes=["blk1"])
        k.op("dve", lambda e: e.memset(eps64[:], EPS), writes=["eps64"])

        def _bq_out(g, row_ap, row_key):
            if g < 4:
                rows_to_cols(row_ap, row_key, 512, bq_cols, "bq_cols", g * 4)
            else:
                k.op("act", lambda e: e.copy(out=bv_row[:, (g - 4) * 512:(g - 3) * 512], in_=row_ap),
                     reads=[row_key], writes=[("bv_row", g - 4)])

        bias_rows(shT[(1, "s1")], ("shT", 1, "s1"), lambda c, g: wq_t[:, c, g * 512:(g + 1) * 512],
                  lambda g: WQ_KEYS, 3 * D, _bq_out)

        xq_t = sb("xq_t", [128, 8, D], F32, QK)
        xq_ring = Ring("xq", xq_t, 8)
        hTq = sb("hTq", [128, 2, 8, 512], BF16, QK)
        kt_t = sb("kt_t", [128, 3, 512], F32, QK)
        kt_ring = Ring("kt", kt_t, 3)
        sq_t = sb("sq_t", [128, 3, 512], BF16, QK)
        sq_ring = Ring("sq", sq_t, 3)
        rs_t = sb("rs_t", [128, 2, 512], F32, QK)
        rs_ring = Ring("rs", rs_t, 2)
        kn_t = sb("kn_t", [128, 4, 512], BF16, QK)
        kn_ring = Ring("kn", kn_t, 4)
        vp_t = sb("vp_t", [128, 3, 16, 65], BF16, QK)
        vp_ring = Ring("vp", vp_t, 3)
        for vi in range(3):
            k.op("pool", lambda e, vi=vi: e.memset(vp_t[:, vi, :, 64:65], 1.0), writes=[("vp1", vi)])

        qgroups = [([4 * g + t for t in range(4)], 0) for g in range(5)] + [([20, 21], 1)]
        QRANGE = {0: (256, 512), 1: (0, 512), 2: (0, 512), 3: (0, 512), 4: (0, 256)}

        def qk_s0(it, st):
            hb, lo, hi, wcol0, bcol, v, gcol, dst_dram, dkey = it
            wd = hi - lo
            b = nbank()
            for c in range(8):
                k.op("pe", lambda e, c=c: e.matmul(banks[b][:, 0:wd], wq_t[:, c, wcol0:wcol0 + 128], hTq[:, hb, c, lo:hi],
                                                  start=(c == 0), stop=(c == 7)),
                     reads=[("wq", wcol0 // D, c)] + [("hTq", hb, ti) for ti in range(4)], writes=[bkey(b)], signal=(c == 7))
            ki, kkey = kt_ring.next()
            si, skey = sq_ring.next()
            k.op("act", lambda e: e.activation(out=sq_t[:, si, 0:wd], in_=banks[b][:, 0:wd], func=AF.Square,
                                               bias=bq_cols[:, bcol, v:v + 1]),
                 reads=[bkey(b), "bq_cols"], writes=[skey])
            k.op("dve", lambda e: e.tensor_scalar(kt_t[:, ki, 0:wd], banks[b][:, 0:wd], bq_cols[:, bcol, v:v + 1], None,
                                                  op0=ALU.add),
                 reads=[bkey(b), "bq_cols", skey], writes=[kkey])
            st.update(ki=ki, kkey=kkey, si=si, skey=skey)

        def qk_s1(it, st):
            hb, lo, hi, wcol0, bcol, v, gcol, dst_dram, dkey = it
            wd = hi - lo
            ki, kkey, si, skey = st["ki"], st["kkey"], st["si"], st["skey"]
            b2 = nbank()
            k.op("pe", lambda e: e.matmul(banks[b2][:, 0:wd], blk1[:, :], sq_t[:, si, 0:wd], start=True, stop=True),
                 reads=["blk1", skey], writes=[bkey(b2)])
            ri, rkey = rs_ring.next()
            k.op("act", lambda e: e.activation(out=rs_t[:, ri, 0:wd], in_=banks[b2][:, 0:wd], func=AF.Ln,
                                               scale=1.0 / 64, bias=eps64[:, 0:1]),
                 reads=[bkey(b2), "eps64"], writes=[rkey])
            k.op("act", lambda e: e.activation(out=rs_t[:, ri, 0:wd], in_=rs_t[:, ri, 0:wd], func=AF.Exp, scale=-0.5),
                 reads=[rkey], writes=[rkey])
            ni, nkey = kn_ring.next()
            k.op("dve", lambda e: e.scalar_tensor_tensor(out=kn_t[:, ni, 0:wd], in0=kt_t[:, ki, 0:wd], scalar=gcol,
                                                         in1=rs_t[:, ri, 0:wd], op0=ALU.mult, op1=ALU.mult),
                 reads=[kkey, rkey, "qkg"], writes=[nkey])
            k.dma("sp", dst_dram, kn_t[:, ni, 0:wd], reads=[nkey], writes=[dkey])

        S1x1, S1x1k = BC[(1, "c1", 0)]
        S1c1, S1c1k = BC[(1, "c1", 1)]
        xq_tiles = {}

        def q_load(gi):
            tiles, v = qgroups[gi]
            for t in tiles:
                xi, xkey = xq_ring.next()
                k.dma("sp", xq_t[:, xi, :], xs_d[t, :, :], reads=[("xs", t)], writes=[xkey])
                xq_tiles[(gi, t)] = (xi, xkey)

        def q_norm_pre(gi):
            tiles, v = qgroups[gi]
            S_ap, skey_ = (S1x1, S1x1k) if v == 0 else (S1c1, S1c1k)
            out = []
            for t in tiles:
                xi, xkey = xq_tiles[(gi, t)]
                x_ap = xq_t[:, xi, :]
                si, skey_stat = stat_ring.next()
                ssq = stat[:, si, 0:1]
                rstd = stat[:, si, 1:2]
                k.op("act", lambda e: e.activation(out=junk[:], in_=x_ap, func=AF.Square, accum_out=ssq),
                     reads=[xkey], writes=["junk", skey_stat])
                k.op("act", lambda e: e.activation(out=rstd, in_=ssq, func=AF.Ln, scale=1.0 / D, bias=epsc[:, 0:1]),
                     reads=[skey_stat, "epsc"], writes=[skey_stat])
                k.op("act", lambda e: e.activation(out=rstd, in_=rstd, func=AF.Exp, scale=-0.5),
                     reads=[skey_stat], writes=[skey_stat])
                qi, qkey = xs_ring.next()
                k.op("dve", lambda e: e.scalar_tensor_tensor(out=xs_t[:, qi, :], in0=x_ap, scalar=rstd, in1=S_ap,
                                                             op0=ALU.mult, op1=ALU.mult),
                     reads=[xkey, skey_stat, skey_], writes=[qkey])
                out.append((qi, qkey))
            return out

        def q_norm_T(gi, pre):
            hb = gi % 2
            for ti, (qi, qkey) in enumerate(pre):
                b = nbank()
                bv = banks[b][:].bitcast(BF16).rearrange("p (c t) -> p c t", c=8)
                for c in range(8):
                    k.op("pe", lambda e, c=c: e.transpose(bv[:, c, :], xs_t[:, qi, c * 128:(c + 1) * 128], identb[:]),
                         reads=[qkey, "identb"], writes=[bkey(b)], signal=(c == 7))
                copy_op(evac_eng(), hTq[:, hb, :, ti * 128:(ti + 1) * 128], bv, [bkey(b)], [("hTq", hb, ti)])

        q_load(0)
        pre = q_norm_pre(0)
        q_norm_T(0, pre)
        for gi, (tiles, v) in enumerate(qgroups):
            hb = gi % 2
            ntok = len(tiles) * 128
            tok0 = (tiles[0]) * 128
            if gi + 1 < len(qgroups):
                q_load(gi + 1)
            its = []
            for kc in range(8):
                its.append((hb, 0, ntok, D + kc * 128, 8 + kc, v, qkg_t[:, 1:2], KTd[kc, :, tok0:tok0 + ntok], ("KTd", kc, gi)))
            if gi in QRANGE:
                lo, hi = QRANGE[gi]
                q0 = tok0 + lo - 256
                for qc in range(8):
                    its.append((hb, lo, hi, qc * 128, qc, v, qkg_t[:, 0:1], QTd[qc, :, q0:q0 + (hi - lo)], ("QTd", qc, gi)))
            run_pipeline(len(its), [lambda i_, st_: qk_s0(its[i_], st_), lambda i_, st_: qk_s1(its[i_], st_)], [0, 1])
            if gi == 0:
                eb_phase()
            if gi + 1 < len(qgroups):
                pre = q_norm_pre(gi + 1)
            for ti, t in enumerate(tiles):
                vi, vkey = vp_ring.next()
                for nh in range(2):
                    b = nbank()
                    for c in range(8):
                        k.op("pe", lambda e, c=c: e.matmul(banks[b][:, :], hTq[:, hb, c, ti * 128:(ti + 1) * 128],
                                                          wq_t[:, c, 2 * D + nh * 512:2 * D + (nh + 1) * 512],
                                                          start=(c == 0), stop=False),
                             reads=[("wq", 2, c), ("hTq", hb, ti)], writes=[bkey(b)], signal=False)
                    k.op("pe", lambda e: e.matmul(banks[b][:, :], sel2b[:, v, :], bv_row[:, nh * 512:(nh + 1) * 512],
                                                  start=False, stop=True),
                         reads=["sel2b", ("bv_row", nh)], writes=[bkey(b)])
                    copy_op(evac_eng(), vp_t[:, vi, nh * 8:(nh + 1) * 8, 0:64],
                            banks[b][:, :].rearrange("p (h d) -> p h d", d=64), [bkey(b), ("vp1", vi)], [(vkey, nh)])
                k.dma("sp", Vd[:, t, :, :].rearrange("h k c -> k h c"),
                      vp_t[:, vi, :, :].rearrange("p (hp two) d -> p hp (two d)", two=2),
                      reads=[(vkey, 0), (vkey, 1)], writes=[("Vd", t)])
            if gi + 1 < len(qgroups):
                q_norm_T(gi + 1, pre)
        k.barrier()
        QK.close()
        AT.close()
        KTD_KEYS = lambda hp: [("KTd", hp, gi) for gi in range(6)]
        QTD_KEYS = lambda hp: [("QTd", hp, gi) for gi in range(5)]
        VD_KEYS = [("Vd", t) for t in range(NALL)]
        dbg_dump("QTd", QTd[:, :, :], [8, 128, 2048], BF16, [kk_ for hp in range(8) for kk_ in QTD_KEYS(hp)])
        dbg_dump("KTd", KTd[:, :, :], [8, 128, NALL * 128], BF16, [kk_ for hp in range(8) for kk_ in KTD_KEYS(hp)])
        dbg_dump("Vd", Vd[:, :, :, :], [8, NALL, 128, 130], BF16, VD_KEYS)
        if stop == "QKV":
            k.finish()
            return nc, dbg_out

        ATT = es.enter_context(ExitStack())
        attn_tok = sb("attn_tok", [128, 16, D], BF16, ATT)
        ATN = es.enter_context(ExitStack())
        rm_t = sb("rm_t", [128, 4, 6, 128], BF16, ATN)
        k.dma("pool", rm_t[:], RM_d[:, :, :, :], writes=["rm"])
        qt_t = sb("qt_t", [128, 2, 2048], BF16, ATN)
        ktt_t = sb("ktt_t", [128, 2, NALL * 128], BF16, ATN)
        vv_t = sb("vv_t", [128, 2, NALL, 130], BF16, ATN)
        eb_t = sb("eb_t", [128, 2, 2, 12, 128], BF16, ATN)
        pt_t = sb("pt_t", [128, 4, 1024], BF16, ATN)
        pt_ring = Ring("pt", pt_t, 4)
        rec_t = sb("rec_t", [128, 4, 2], F32, ATN)
        rec_ring = Ring("rec", rec_t, 4)
        CLS = {0: 0, 1: 1, 14: 2, 15: 3}

        def load_pair(hp):
            s = hp % 2
            k.dma("sp", qt_t[:, s, :], QTd[hp, :, :], reads=QTD_KEYS(hp), writes=[("qt", s)])
            k.dma("sp", ktt_t[:, s, :], KTd[hp, :, :], reads=KTD_KEYS(hp), writes=[("ktt", s)])
            k.dma("sp", vv_t[:, s, :, :], Vd[hp, :, :, :].rearrange("t k c -> k t c"), reads=VD_KEYS, writes=[("vv", s)])
            k.dma("sp", eb_t[:, s, :, :, :], EBd[hp, :, :, :, :], reads=[("EBd", 2 * hp), ("EBd", 2 * hp + 1)],
                  writes=[("eb", s)])

        load_pair(0)
        items = [(hp, qb, hh) for hp in range(8) for qb in range(16) for hh in range(2)]

        def tile_list(qb):
            if qb == 0:
                return list(range(0, 6)), 1
            if qb == 15:
                return list(range(14, 20)), 0
            if qb in (1, 14):
                return list(range(qb, qb + 5)), 1
            return list(range(qb, qb + 5)), 7

        obank = {}

        def a_s0(i, st):
            hp, qb, hh = items[i]
            s_ = hp % 2
            tl, s0 = tile_list(qb)
            tiles_all = tl + [20, 21]
            nt = len(tiles_all)
            pr = slice(hh * 64, (hh + 1) * 64)
            bA, bB = nbank(), nbank()
            for si_, t in enumerate(tiles_all):
                bb_ = bA if si_ < 4 else bB
                col = (si_ % 4) * 128
                last = (si_ == 3 or si_ == nt - 1)
                k.op("pe", lambda e, t=t, bb_=bb_, col=col: e.matmul(
                    banks[bb_][:, col:col + 128], ktt_t[pr, s_, t * 128:(t + 1) * 128],
                    qt_t[pr, s_, qb * 128:(qb + 1) * 128], start=True, stop=True),
                    reads=[("ktt", s_), ("qt", s_)], writes=[bkey(bb_)], signal=last)
            st.update(bA=bA, bB=bB, tiles_all=tiles_all, nt=nt, nloc=len(tl), s0=s0)

        def a_s1(i, st):
            hp, qb, hh = items[i]
            s_ = hp % 2
            bA, bB, nt, nloc, s0 = st["bA"], st["bB"], st["nt"], st["nloc"], st["s0"]
            pi, pkey = pt_ring.next()
            st.update(pi=pi, pkey=pkey)
            k.op("act", lambda e: e.activation(out=pt_t[:, pi, 0:512], in_=banks[bA][:, :], func=AF.Exp),
                 reads=[bkey(bA)], writes=[(pkey, 0)])
            nB = (nt - 4) * 128
            k.op("act", lambda e: e.activation(out=pt_t[:, pi, 512:512 + nB], in_=banks[bB][:, 0:nB], func=AF.Exp),
                 reads=[bkey(bB)], writes=[(pkey, 1)])
            k.op("dve", lambda e: e.tensor_tensor(
                out=pt_t[:, pi, 0:nloc * 128], in0=pt_t[:, pi, 0:nloc * 128],
                in1=eb_t[:, s_, hh, s0:s0 + nloc, :].rearrange("p a b -> p (a b)"), op=ALU.mult),
                reads=[(pkey, 0), (pkey, 1), ("eb", s_)], writes=[(pkey, 0), (pkey, 1)])
            if qb in CLS:
                k.op("pool", lambda e: e.tensor_tensor(
                    out=pt_t[:, pi, 0:nloc * 128], in0=pt_t[:, pi, 0:nloc * 128],
                    in1=rm_t[:, CLS[qb], 0:nloc, :].rearrange("p a b -> p (a b)"), op=ALU.mult),
                    reads=[(pkey, 0), (pkey, 1), "rm"], writes=[(pkey, 0), (pkey, 1)])

        def a_s2(i, st):
            hp, qb, hh = items[i]
            s_ = hp % 2
            pi, pkey, tiles_all, nt = st["pi"], st["pkey"], st["tiles_all"], st["nt"]
            if hh == 0:
                obank[(hp, qb)] = nbank()
            bO = obank[(hp, qb)]
            for si_, t in enumerate(tiles_all):
                k.op("pe", lambda e, t=t, si_=si_: e.matmul(
                    banks[bO][:, hh * 65:(hh + 1) * 65], pt_t[:, pi, si_ * 128:(si_ + 1) * 128],
                    vv_t[:, s_, t, hh * 65:(hh + 1) * 65], start=(si_ == 0), stop=(si_ == nt - 1)),
                    reads=[(pkey, 0), (pkey, 1), ("vv", s_)], writes=[bkey(bO)], signal=(si_ == nt - 1))
            if hh == 1:
                ri, rkey = rec_ring.next()
                Ov = banks[bO][:, 0:130].rearrange("p (h c) -> p h c", c=65)
                k.op("dve", lambda e: e.reciprocal(out=rec_t[:, ri, :], in_=Ov[:, :, 64]), reads=[bkey(bO)], writes=[rkey])
                k.op("dve", lambda e: e.tensor_tensor(
                    out=attn_tok[:, qb, hp * 128:(hp + 1) * 128].rearrange("p (h d) -> p h d", d=64),
                    in0=Ov[:, :, 0:64], in1=rec_t[:, ri, :].unsqueeze(2).to_broadcast([128, 2, 64]), op=ALU.mult),
                    reads=[bkey(bO), rkey], writes=[("attn", qb, hp)])

        for hp_ in range(8):
            if hp_ + 1 < 8:
                load_pair(hp_ + 1)
            run_pipeline(32, [lambda i_, st_: a_s0(hp_ * 32 + i_, st_), lambda i_, st_: a_s1(hp_ * 32 + i_, st_),
                              lambda i_, st_: a_s2(hp_ * 32 + i_, st_)], [0, 1, 2])
            if hp_ == 0:
                modulation(1, [2, 3, 4, 5], [0], bc1_alloc)
        k.barrier()
        ATN.close()
        dbg_dump("attn", attn_tok[:, :, :], [128, 16, D], BF16, [("attn", qb, hp) for qb in range(16) for hp in range(8)])

        AO = es.enter_context(ExitStack())
        Wo2 = sb("Wo2", [128, 8, D], BF16, AO)
        load_w_bf16(Wo2, wnao_b, "Wo2", WNAO_KEYS)
        aT_t = sb("aT_t", [128, 2, 8, 128], BF16, AO)
        aT_ring = Ring("aT", aT_t, 2)
        xa_t = sb("xa_t", [128, 4, D], F32, AO)
        xa_ring = Ring("xa", xa_t, 4)
        ga_t = sb("ga_t", [128, 2, 512], F32, AO)
        ga_ring = Ring("ga", ga_t, 2)
        G1x1, G1x1k = BC[(1, "g1", 0)]
        aslots = {}

        def a_load(qb):
            xi, xkey = xa_ring.next()
            k.dma("sp", xa_t[:, xi, :], xs_d[qb + 2, :, :], reads=[("xs", qb + 2)], writes=[xkey])
            aslots[qb] = (xi, xkey)

        a_load(0)
        a_load(1)
        for qb in range(16):
            t = qb + 2
            if qb + 2 < 16:
                a_load(qb + 2)
            xi, xkey = aslots[qb]
            b = nbank()
            bv = banks[b][:].bitcast(BF16).rearrange("p (c t) -> p c t", c=8)
            for c in range(8):
                k.op("pe", lambda e, c=c: e.transpose(bv[:, c, :], attn_tok[:, qb, c * 128:(c + 1) * 128], identb[:]),
                     reads=[("attn", qb, c), "identb"], writes=[bkey(b)], signal=(c == 7))
            ai, akey = aT_ring.next()
            copy_op(evac_eng(), aT_t[:, ai, :, :], bv, [bkey(b)], [akey])
            for nh in range(2):
                b2 = nbank()
                for kc in range(8):
                    k.op("pe", lambda e, kc=kc: e.matmul(banks[b2][:, :], aT_t[:, ai, kc, :],
                                                        Wo2[:, kc, nh * 512:(nh + 1) * 512], start=(kc == 0), stop=(kc == 7)),
                         reads=[akey, ("Wo2", kc)], writes=[bkey(b2)], signal=(kc == 7))
                gi_, gkey_ = ga_ring.next()
                k.op("dve", lambda e: e.tensor_tensor(out=ga_t[:, gi_, :], in0=banks[b2][:, :],
                                                      in1=G1x1[:, nh * 512:(nh + 1) * 512], op=ALU.mult),
                     reads=[bkey(b2), G1x1k], writes=[gkey_])
                k.op("pool", lambda e: e.tensor_tensor(out=xa_t[:, xi, nh * 512:(nh + 1) * 512],
                                                       in0=xa_t[:, xi, nh * 512:(nh + 1) * 512], in1=ga_t[:, gi_, :], op=ALU.add),
                     reads=[gkey_, xkey], writes=[xkey])
            k.dma("act", xs_d[t, :, :], xa_t[:, xi, :], reads=[xkey], writes=[("xs", t)])
        k.barrier()
        AO.close()
        ATT.close()
        dbg_dump("x2m", xs_d[2:18, :, :], [16, 128, D], F32, [("xs", t) for t in range(2, 18)])
        if stop == "ATT":
            k.finish()
            return nc, dbg_out

        ML1 = es.enter_context(ExitStack())
        groups1 = [([2 + 4 * g + t for t in range(4)], 0) for g in range(4)]
        mlp_layer(1, groups1, lambda v: BC[(1, "c2", v)], lambda v: BC[(1, "g2", v)],
                  lambda t: y[(t - 2) * 128:(t - 1) * 128, :], lambda t: ("y", t), ML1)
        k.finish()
        print("ops", k.n_ops, "waits", k.n_waits)
    return nc, dbg_out


def host_consts(jq):
    rot = 32 * jq - 4
    grow = (rot + np.arange(128)) % 128
    t2 = np.arange(64)
    k1 = np.arange(128)
    tok = 64 * grow[:, None] + t2[None, :]
    ang = 2 * np.pi * ((tok[:, :, None] * k1[None, None, :]) % 8192) / 8192.0
    MT = np.stack([np.cos(ang), -np.sin(ang)], axis=2).astype(np.float32)
    k2 = (16 * jq - 2 + np.arange(20)) % 64
    a = 2 * np.pi * ((t2[:, None] * k2[None, :]) % 64) / 64.0
    s = 1.0 / np.sqrt(8192.0)
    WC = np.zeros((128, 2, 20))
    WC[0:64, 0] = np.cos(a) * s
    WC[0:64, 1] = -np.sin(a) * s
    WC[64:128, 0] = np.sin(a) * s
    WC[64:128, 1] = np.cos(a) * s
    c = np.arange(64)
    b = 2 * np.pi * ((c[:, None] * c[None, :]) % 64) / 64.0
    BCS = np.zeros((128, 2, 128))
    for gl in range(2):
        BCS[gl * 64:(gl + 1) * 64, 0, gl * 64:(gl + 1) * 64] = np.cos(b) / 8.0
        BCS[gl * 64:(gl + 1) * 64, 1, gl * 64:(gl + 1) * 64] = np.sin(b) / 8.0
    t = (np.arange(2)[None, :] * 128 + np.arange(128)[:, None])
    kk = np.arange(256)
    a2 = 2 * np.pi * ((t[:, :, None] * kk[None, None, :]) % 256) / 256.0
    CT = np.stack([np.cos(a2), -np.sin(a2)], axis=2) / 16.0
    gmask = np.ones((128, 2))
    if jq == 0:
        gmask[:, 0] = 0.0
    if jq == 3:
        gmask[:, 1] = 0.0
    return dict(MT=MT, WC=WC.astype(np.float32), BCS=BCS.astype(np.float32), CT256=CT.astype(np.float32),
                gmask=gmask.astype(np.float32))


def make_in_maps(inputs):
    x = np.asarray(inputs["x"], np.float32)
    c = np.asarray(inputs["c"], np.float32)
    ctx = np.asarray(inputs["ctx"], np.float32)
    c_ctx = np.asarray(inputs["c_ctx"], np.float32)
    shared = dict(
        ada_w=np.ascontiguousarray(inputs["ada_w"], np.float32),
        ada_b=np.ascontiguousarray(inputs["ada_b"], np.float32),
        nmix=np.ascontiguousarray(inputs["norm_mix_g"], np.float32),
        nmlp=np.ascontiguousarray(inputs["norm_mlp_g"], np.float32),
        w1=np.ascontiguousarray(inputs["mlp_w1"], np.float32),
        w2=np.ascontiguousarray(inputs["mlp_w2"], np.float32),
        w_in=np.ascontiguousarray(inputs["ab_w_in"][0], np.float32),
        convw=np.ascontiguousarray(np.asarray(inputs["ab_conv_w"][0], np.float32).reshape(3, 4, 128).transpose(2, 1, 0)),
        w_abo=np.ascontiguousarray(inputs["ab_w_out"][0], np.float32),
        w_qkv=np.ascontiguousarray(inputs["na_w_qkv"][0], np.float32),
        qkg=np.ascontiguousarray(np.stack([np.tile(np.asarray(inputs["na_q_g"][0], np.float32), 2),
                                           np.tile(np.asarray(inputs["na_k_g"][0], np.float32), 2)], axis=1)),
        w_nao=np.ascontiguousarray(inputs["na_w_out"][0], np.float32),
        ident=np.eye(128, dtype=np.float32),
        sel=np.stack([np.stack([np.ones(128), np.zeros(128)]),
                      np.stack([np.zeros(128), np.ones(128)])]).astype(np.float32),
    )
    maps = []
    for j in range(NCORES):
        b, jq = j // 4, j % 4
        rot = 32 * jq - 4
        xb = x[b].reshape(128, 64, D)
        m = dict(shared)
        m["xf"] = np.ascontiguousarray(np.roll(xb, -rot, axis=0))
        m["ctxb"] = np.ascontiguousarray(ctx[b])
        cond = np.stack([c[b], c_ctx], axis=1)
        m["condT"] = np.ascontiguousarray(cond.reshape(8, 128, 2).transpose(1, 0, 2))
        m.update(host_consts(jq))
        m.update(attn_tables(inputs, jq))
        maps.append(m)
    return maps


def attn_tables(inputs, jq):
    rpb = np.asarray(inputs["na_rpb"][0], np.float32)
    R0 = 32 * jq
    kr = np.arange(128) // 64
    kc = np.arange(128) % 64
    qr = np.arange(128) // 64
    qc = np.arange(128) % 64
    cs = np.clip(qc - 8, 0, 48)
    colok = (kc[:, None] >= cs[None, :]) & (kc[:, None] < cs[None, :] + 16)
    dc = np.clip(kc[:, None] - qc[None, :] + 15, 0, 30)
    slots = [(-3, False), (-2, False), (-1, False), (0, False), (1, False), (2, False), (3, False),
             (-2, True), (-1, True), (0, True), (1, True), (2, True)]
    ebias = np.zeros((128, 16, 12, 128), np.float32)
    emask = np.zeros((128, 12, 128), np.float32)
    for si, (dl, interior) in enumerate(slots):
        dr = 2 * dl + kr[:, None] - qr[None, :]
        ok = colok & (np.abs(dr) <= 7)
        if interior:
            ok = ok & (dr >= -4) & (dr <= 3)
        ro = np.clip(dr + 7, 0, 14)
        g = rpb[:, ro, dc]
        ebias[:, :, si, :] = np.where(ok[None], g, np.float32(0)).transpose(1, 0, 2)
        emask[:, si, :] = ok
    rmask = np.zeros((128, 4, 6, 128), np.float32)
    for cls, qb in enumerate((0, 1, 14, 15)):
        if qb == 0:
            tl = list(range(0, 6))
        elif qb == 15:
            tl = list(range(14, 20))
        else:
            tl = list(range(qb, qb + 5))
        r = R0 + 2 * qb + qr
        rs = np.clip(r - 4, 0, 120)
        for idx, t in enumerate(tl):
            gk = R0 - 4 + 2 * t + kr
            ok = (gk[:, None] >= 0) & (gk[:, None] <= 127) & (gk[:, None] >= rs[None, :]) & (gk[:, None] < rs[None, :] + 8)
            rmask[:, cls, idx, :] = ok
    return dict(ebias=ebias, emask=emask, rmask=rmask)


def kernel(**inputs):
    nc, _ = build()
    maps = make_in_maps(inputs)
    res = run_bass_kernel_spmd(nc, maps, core_ids=list(range(NCORES)))
    out = np.zeros((2, 8192, D), np.float32)
    for j in range(NCORES):
        b, jq = j // 4, j % 4
        out[b, jq * 2048:(jq + 1) * 2048] = res.results[j]["y"]
    return out
```
